# Optimizing a Trainium2 kernel written in Bass

```python
import jax, jax.numpy as jnp
from jax import lax
import numpy as np

D_MODEL = 2048
BATCH = 4
SEQ = 2048
DEPTH = 4
DEC_BATCH = 8
DEC_SEQ = 1
PAST_LEN = 16384
PAGE_SIZE = 128

W_A = 1024
CONV_W = 3
W_B = 1024
CHUNK = 128
N_B_GROUPS = 4
N_C_HEADS = 4
HEAD_DIM = 128
ROT_DIM = HEAD_DIM // 4
ROPE_THETA = 500000.0
DIL_GROUPS = ((128, 1), (512, 4), (2048, 16))
N_DIL = len(DIL_GROUPS)
QB = 128
C_WIDTH = N_C_HEADS * HEAD_DIM
QKV_W = N_DIL * C_WIDTH
D_FF = ((-(-8 * D_MODEL // 3) + 255) // 256) * 256
IN_WIDTH = 3 * W_A + 2 * W_B + 3 * QKV_W + 3 * D_MODEL
EPS = 1e-6

kernel_name = "hybrid_gated_conv_gmlp_dilated_attn_decode_step"


def rmsnorm(x, g):
    xf = x.astype(jnp.float32)
    y = xf * lax.rsqrt(jnp.mean(xf * xf, axis=-1, keepdims=True) + EPS)
    return (y * g.astype(jnp.float32)).astype(x.dtype)


def layernorm(x, g, b):
    xf = x.astype(jnp.float32)
    mu = jnp.mean(xf, axis=-1, keepdims=True)
    xc = xf - mu
    var = jnp.mean(xc * xc, axis=-1, keepdims=True)
    y = xc * lax.rsqrt(var + EPS) * g.astype(jnp.float32) + b.astype(jnp.float32)
    return y.astype(x.dtype)


def rope(x, pos):
    inv_freq = ROPE_THETA ** (-jnp.arange(0, ROT_DIM, 2, dtype=jnp.float32) / ROT_DIM)
    ang = pos.astype(jnp.float32)[:, None] * inv_freq[None, :]
    cos = jnp.cos(ang)[None, :, None, :]
    sin = jnp.sin(ang)[None, :, None, :]
    xf = x.astype(jnp.float32)
    x1 = xf[..., : ROT_DIM // 2]
    x2 = xf[..., ROT_DIM // 2: ROT_DIM]
    out = jnp.concatenate([x1 * cos - x2 * sin, x2 * cos + x1 * sin, xf[..., ROT_DIM:]], axis=-1)
    return out.astype(x.dtype)


def split_proj(h, w_in):
    sizes = (W_A, W_A, W_A, W_B, W_B, QKV_W, QKV_W, QKV_W, D_MODEL, D_MODEL, D_MODEL)
    points, acc = [], 0
    for s in sizes[:-1]:
        acc += s
        points.append(acc)
    proj = jnp.einsum('bsd,de->bse', h, w_in)
    return jnp.split(proj, points, axis=-1)


def dwconv(z_ext, conv_w):
    T = z_ext.shape[1] - (CONV_W - 1)
    out = conv_w[0] * z_ext[:, 0:T]
    for j in range(1, CONV_W):
        out = out + conv_w[j] * z_ext[:, j:j + T]
    return out


def spatial_gate_prompt(vn, w_s, b_s):
    B, S, _ = vn.shape
    vc = vn.reshape(B, S // CHUNK, CHUNK, N_B_GROUPS, W_B // N_B_GROUPS)
    s = jnp.einsum('gij,bcjgd->bcigd', jnp.tril(w_s), vc) + b_s.T[None, None, :, :, None]
    return s.reshape(B, S, W_B)


def spatial_gate_sample(vn, w_s, b_s):
    B, T, _ = vn.shape
    vc = vn.reshape(B, T, N_B_GROUPS, W_B // N_B_GROUPS)
    s = jnp.einsum('gij,bjgd->bigd', jnp.tril(w_s[:, :T, :T]), vc) + b_s[:, :T].T[None, :, :, None]
    return s.reshape(B, T, W_B)


def banded_attn(q, k, v, n_back):
    N, L, H, hd = q.shape
    nb = L // QB
    qb = q.reshape(N, nb, QB, H, hd)

    def with_prev(t):
        tb = t.reshape(N, nb, QB, H, hd)
        tp = jnp.concatenate([jnp.zeros_like(tb[:, :1]), tb], axis=1)
        return jnp.concatenate([tp[:, :-1], tp[:, 1:]], axis=2)

    k2, v2 = with_prev(k), with_prev(v)
    s = jnp.einsum('ncqhd,nckhd->nhcqk', qb, k2, preferred_element_type=jnp.float32) * (hd ** -0.5)
    blk = jnp.arange(nb)[:, None]
    qpos = blk * QB + jnp.arange(QB)[None, :]
    kpos = (blk - 1) * QB + jnp.arange(2 * QB)[None, :]
    dist = qpos[:, :, None] - kpos[:, None, :]
    mask = (dist >= 0) & (dist <= n_back) & (kpos[:, None, :] >= 0)
    s = jnp.where(mask, s, -jnp.inf)
    lse = jax.nn.logsumexp(s, axis=-1)
    p = jnp.exp(s - lse[..., None])
    o = jnp.einsum('nhcqk,nckhd->ncqhd', p.astype(v.dtype), v2).reshape(N, L, H, hd)
    return o, lse.transpose(0, 2, 3, 1).reshape(N, L, H)


def dilated_attn_prompt(q, k, v, window, dil):
    B, S, H, hd = q.shape
    L = S // dil
    Lp = -(-L // QB) * QB

    def to_classes(t):
        t = t.reshape(B, L, dil, H, hd).transpose(0, 2, 1, 3, 4).reshape(B * dil, L, H, hd)
        return jnp.pad(t, ((0, 0), (0, Lp - L), (0, 0), (0, 0)))

    o, lse = banded_attn(to_classes(q), to_classes(k), to_classes(v), window // dil)
    o = o[:, :L].reshape(B, dil, L, H, hd).transpose(0, 2, 1, 3, 4).reshape(B, S, H, hd)
    lse = lse[:, :L].reshape(B, dil, L, H).transpose(0, 2, 1, 3).reshape(B, S, H)
    return o, lse


def dilated_attn_sample(q, k_new, v_new, kv_buf, window, dil):
    Lbuf = kv_buf.shape[1]
    T = q.shape[1]
    k_all = jnp.concatenate([kv_buf[:, :, 0].astype(k_new.dtype), k_new], axis=1)
    v_all = jnp.concatenate([kv_buf[:, :, 1].astype(v_new.dtype), v_new], axis=1)
    n_keys = window // dil + 1
    idx = Lbuf + jnp.arange(T)[:, None] - dil * jnp.arange(n_keys)[None, :]
    valid = idx >= 0
    idx = jnp.maximum(idx, 0)
    kg = jnp.take(k_all, idx, axis=1)
    vg = jnp.take(v_all, idx, axis=1)
    s = jnp.einsum('bthd,btjhd->bhtj', q, kg, preferred_element_type=jnp.float32) * (HEAD_DIM ** -0.5)
    s = jnp.where(valid[None, None], s, -jnp.inf)
    lse = jax.nn.logsumexp(s, axis=-1)
    p = jnp.exp(s - lse[..., None])
    o = jnp.einsum('bhtj,btjhd->bthd', p.astype(vg.dtype), vg)
    return o, lse.transpose(0, 2, 1)


def combine_groups(outs, lses):
    w = jax.nn.softmax(jnp.stack(lses, axis=0), axis=0)
    o = jnp.einsum('gbsh,gbshd->bshd', w, jnp.stack(outs, axis=0).astype(jnp.float32))
    B, S = o.shape[:2]
    return o.reshape(B, S, C_WIDTH).astype(outs[0].dtype)


def mixer_heads(q, k, v, pos):
    B, S, _ = q.shape
    q = rope(q.reshape(B, S, N_DIL * N_C_HEADS, HEAD_DIM), pos).reshape(B, S, N_DIL, N_C_HEADS, HEAD_DIM)
    k = rope(k.reshape(B, S, N_DIL * N_C_HEADS, HEAD_DIM), pos).reshape(B, S, N_DIL, N_C_HEADS, HEAD_DIM)
    v = v.reshape(B, S, N_DIL, N_C_HEADS, HEAD_DIM)
    return q, k, v


def merge_out(x, y_a, y_b, y_c, ga, gb, gc, lp):
    m = (jax.nn.sigmoid(ga) * (y_a @ lp['w_a_out'])
         + jax.nn.sigmoid(gb) * (y_b @ lp['w_b_out'])
         + jax.nn.sigmoid(gc) * (y_c @ lp['w_c_out']))
    return x + rmsnorm(m @ lp['w_o'], lp['g_post_mix'])


def ffn_block(x, lp):
    h = rmsnorm(x, lp['g_pre_ffn'])
    gate, up = jnp.split(h @ lp['w_ffn_in'], 2, axis=-1)
    f = (jax.nn.silu(gate) * up) @ lp['w_ffn_out']
    return x + rmsnorm(f, lp['g_post_ffn'])


def layer_prompt(x, lp):
    S = x.shape[1]
    h = rmsnorm(x, lp['g_pre_mix'])
    a_b, a_c, a_x, b_u, b_v, q, k, v, ga, gb, gc = split_proj(h, lp['w_in'])
    z = a_c * a_x
    z_ext = jnp.pad(z, ((0, 0), (CONV_W - 1, 0), (0, 0)))
    y_a = a_b * dwconv(z_ext, lp['conv_w'])
    conv_state = z_ext[:, -(CONV_W - 1):]
    vn = layernorm(b_v, lp['ln_g'], lp['ln_b'])
    y_b = b_u * spatial_gate_prompt(vn, lp['w_s'], lp['b_s'])
    q, k, v = mixer_heads(q, k, v, jnp.arange(S, dtype=jnp.int32))
    outs, lses, kv_states = [], [], []
    for g, (win, dil) in enumerate(DIL_GROUPS):
        o, lse = dilated_attn_prompt(q[:, :, g], k[:, :, g], v[:, :, g], win, dil)
        outs.append(o)
        lses.append(lse)
        keep = min(win, S)
        kv_states.append(jnp.stack([k[:, S - keep:, g], v[:, S - keep:, g]], axis=2))
    y_c = combine_groups(outs, lses)
    x = merge_out(x, y_a, y_b, y_c, ga, gb, gc, lp)
    x = ffn_block(x, lp)
    return x, conv_state, kv_states


def layer_sample(x, conv_buf, kv_bufs, lp):
    T = x.shape[1]
    h = rmsnorm(x, lp['g_pre_mix'])
    a_b, a_c, a_x, b_u, b_v, q, k, v, ga, gb, gc = split_proj(h, lp['w_in'])
    z = a_c * a_x
    z_ext = jnp.concatenate([conv_buf.astype(z.dtype), z], axis=1)
    y_a = a_b * dwconv(z_ext, lp['conv_w'])
    conv_state = z_ext[:, -(CONV_W - 1):]
    vn = layernorm(b_v, lp['ln_g'], lp['ln_b'])
    y_b = b_u * spatial_gate_sample(vn, lp['w_s'], lp['b_s'])
    q, k, v = mixer_heads(q, k, v, PAST_LEN + jnp.arange(T, dtype=jnp.int32))
    outs, lses, kv_rows = [], [], []
    for g, (win, dil) in enumerate(DIL_GROUPS):
        o, lse = dilated_attn_sample(q[:, :, g], k[:, :, g], v[:, :, g], kv_bufs[g], win, dil)
        outs.append(o)
        lses.append(lse)
        kv_rows.append(jnp.stack([k[:, :, g], v[:, :, g]], axis=2))
    y_c = combine_groups(outs, lses)
    x = merge_out(x, y_a, y_b, y_c, ga, gb, gc, lp)
    x = ffn_block(x, lp)
    return x, conv_state, kv_rows, vn


def setup_inputs(seed: int = 0) -> dict:
    key = jax.random.key(seed)
    ks = jax.random.split(key, 24)
    nrm = jax.random.normal
    f32 = jnp.float32
    d = D_MODEL
    inp = {}
    inp['x_prompt'] = nrm(ks[0], (BATCH, SEQ, d), f32)
    inp['x_sample'] = nrm(ks[1], (DEC_BATCH, DEC_SEQ, d), f32)
    inp['state_conv'] = nrm(ks[2], (DEPTH, DEC_BATCH, CONV_W - 1, W_A), f32)
    inp['cache_kv_w128'] = nrm(ks[3], (DEPTH, DEC_BATCH, min(DIL_GROUPS[0][0], PAST_LEN), 2, N_C_HEADS, HEAD_DIM), f32)
    inp['cache_kv_w512'] = nrm(ks[4], (DEPTH, DEC_BATCH, min(DIL_GROUPS[1][0], PAST_LEN), 2, N_C_HEADS, HEAD_DIM), f32)
    inp['cache_kv_w2048'] = nrm(ks[5], (DEPTH, DEC_BATCH, min(DIL_GROUPS[2][0], PAST_LEN), 2, N_C_HEADS, HEAD_DIM), f32)
    inp['g_pre_mix'] = 1.0 + 0.05 * nrm(ks[6], (DEPTH, d), f32)
    inp['w_in'] = nrm(ks[7], (DEPTH, d, IN_WIDTH), f32) * d ** -0.5
    inp['conv_w'] = nrm(ks[8], (DEPTH, CONV_W, W_A), f32) * CONV_W ** -0.5
    inp['ln_g'] = 1.0 + 0.05 * nrm(ks[9], (DEPTH, W_B), f32)
    inp['ln_b'] = 0.05 * nrm(ks[10], (DEPTH, W_B), f32)
    inp['w_s'] = nrm(ks[11], (DEPTH, N_B_GROUPS, CHUNK, CHUNK), f32) * CHUNK ** -0.5
    inp['b_s'] = 1.0 + 0.1 * nrm(ks[12], (DEPTH, N_B_GROUPS, CHUNK), f32)
    inp['w_a_out'] = nrm(ks[13], (DEPTH, W_A, d), f32) * W_A ** -0.5
    inp['w_b_out'] = nrm(ks[14], (DEPTH, W_B, d), f32) * W_B ** -0.5
    inp['w_c_out'] = nrm(ks[15], (DEPTH, C_WIDTH, d), f32) * C_WIDTH ** -0.5
    inp['w_o'] = nrm(ks[16], (DEPTH, d, d), f32) * d ** -0.5
    inp['g_post_mix'] = 1.0 + 0.05 * nrm(ks[17], (DEPTH, d), f32)
    inp['g_pre_ffn'] = 1.0 + 0.05 * nrm(ks[18], (DEPTH, d), f32)
    inp['w_ffn_in'] = nrm(ks[19], (DEPTH, d, 2 * D_FF), f32) * d ** -0.5
    inp['w_ffn_out'] = nrm(ks[20], (DEPTH, D_FF, d), f32) * D_FF ** -0.5
    inp['g_post_ffn'] = 1.0 + 0.05 * nrm(ks[21], (DEPTH, d), f32)
    return inp


def reference(x_prompt, x_sample, state_conv, cache_kv_w128, cache_kv_w512, cache_kv_w2048,
              g_pre_mix, w_in, conv_w, ln_g, ln_b, w_s, b_s, w_a_out, w_b_out, w_c_out, w_o,
              g_post_mix, g_pre_ffn, w_ffn_in, w_ffn_out, g_post_ffn):
    kv_caches = (cache_kv_w128, cache_kv_w512, cache_kv_w2048)
    xp, xs = x_prompt, x_sample
    conv_p, conv_s, vchunk_s = [], [], []
    kv_p = [[] for _ in DIL_GROUPS]
    kv_s = [[] for _ in DIL_GROUPS]
    for l in range(DEPTH):
        lp = dict(g_pre_mix=g_pre_mix[l], w_in=w_in[l], conv_w=conv_w[l], ln_g=ln_g[l], ln_b=ln_b[l],
                  w_s=w_s[l], b_s=b_s[l], w_a_out=w_a_out[l], w_b_out=w_b_out[l], w_c_out=w_c_out[l],
                  w_o=w_o[l], g_post_mix=g_post_mix[l], g_pre_ffn=g_pre_ffn[l], w_ffn_in=w_ffn_in[l],
                  w_ffn_out=w_ffn_out[l], g_post_ffn=g_post_ffn[l])
        xp, cst_p, kvst_p = layer_prompt(xp, lp)
        xs, cst_s, kvrow_s, vn_s = layer_sample(xs, state_conv[l], [c[l] for c in kv_caches], lp)
        conv_p.append(cst_p)
        conv_s.append(cst_s)
        vchunk_s.append(vn_s)
        for g in range(N_DIL):
            kv_p[g].append(kvst_p[g])
            kv_s[g].append(kvrow_s[g])
    y_prompt, y_sample = xp, xs
    conv_state_prompt = jnp.stack(conv_p, axis=0)
    kv_w128_prompt = jnp.stack(kv_p[0], axis=0)
    kv_w512_prompt = jnp.stack(kv_p[1], axis=0)
    kv_w2048_prompt = jnp.stack(kv_p[2], axis=0)
    conv_state_sample = jnp.stack(conv_s, axis=0)
    kv_w128_sample = jnp.stack(kv_s[0], axis=0)
    kv_w512_sample = jnp.stack(kv_s[1], axis=0)
    kv_w2048_sample = jnp.stack(kv_s[2], axis=0)
    v_chunk_sample = jnp.stack(vchunk_s, axis=0)
    return (y_prompt, y_sample, conv_state_prompt, kv_w128_prompt, kv_w512_prompt, kv_w2048_prompt,
            conv_state_sample, kv_w128_sample, kv_w512_sample, kv_w2048_sample, v_chunk_sample)
```

```python
import numpy as np
import concourse.bass as bass
import concourse.mybir as mybir
from concourse.bass_utils import run_bass_kernel_spmd

F32, BF16 = mybir.dt.float32, mybir.dt.bfloat16
AF = mybir.ActivationFunctionType
ALU = mybir.AluOpType

L = 4
D = 2048
T = 1024
TT = 1025
TP = 1028
EPS = 1e-6
O_AB, O_AC, O_AX, O_BU, O_BV, O_Q, O_K, O_V, O_GA, O_GB, O_GC = 0, 1024, 2048, 3072, 4096, 5120, 6656, 8192, 9728, 11776, 13824
DFF = 5632
TILES = [(0, 342), (342, 342), (684, 341)]
POFF = [0, 512, 1024]
NDMA = 24
NWB = 7
SCW = 1152
NSC = 6
PP_G = 0
PP_CW = 64
PP_LNG = 88
PP_LNB = 96
PP_WS0 = 104
PP_BS0 = 112
PPN = 120
SND_ROWS = 1666


class Tok:
    __slots__ = ("eng", "kind", "sem", "val", "need")

    def __init__(self, eng, kind):
        self.eng, self.kind, self.sem, self.val, self.need = eng, kind, None, None, False


class Prog:
    ENGS = ("pe", "act", "dve", "pool", "sp")

    def __init__(self):
        self.streams = {e: [] for e in self.ENGS}
        self.reg = {}
        self.dma_n = 0
        self.dma_last = [None] * (NDMA + NWB)
        self.dma_cnt = [0] * (NDMA + NWB)
        self.cc_cnts = [0, 0, 0, 0, 0]
        self.sc_i = 0

    def scratch(self):
        i = self.sc_i % NSC
        self.sc_i += 1
        return i

    def op(self, eng, fn, rd=(), wr=(), aw=(), kind="c", extra=(), dsem=None):
        deps = []

        def add(t):
            if t is None:
                return
            if t.eng == "pe" and eng == "pe" and t.kind == "c" and kind == "c":
                return
            deps.append(t)

        for r in rd:
            e = self.reg.get(r)
            if e:
                for t in e[0]:
                    add(t)
        for r in list(wr) + list(aw):
            e = self.reg.get(r)
            if e:
                if not (r in aw and kind == "d"):
                    for t in e[0]:
                        add(t)
                for t in e[1].values():
                    add(t)
                for t in e[2]:
                    add(t)
        for t in extra:
            add(t)
        tok = Tok(eng, kind)
        if kind == "d":
            if dsem is None:
                j = self.dma_n % NDMA
                self.dma_n += 1
            else:
                j = NDMA + dsem
            add(self.dma_last[j])
            self.dma_cnt[j] += 1
            tok.sem, tok.val = j, 16 * self.dma_cnt[j]
            self.dma_last[j] = tok
        elif kind == "cc":
            self.cc_cnts[dsem] += 1
            tok.sem, tok.val = dsem, self.cc_cnts[dsem]
        for r in rd:
            e = self.reg.setdefault(r, [[], {}, []])
            if kind == "c":
                e[1][eng] = tok
            else:
                e[2].append(tok)
        for r in wr:
            self.reg[r] = [[tok], {}, []]
        for r in aw:
            e = self.reg.get(r)
            if e and kind == "d" and not e[1] and not e[2]:
                e[0].append(tok)
            else:
                self.reg[r] = [[tok], {}, []]
        for t in deps:
            t.need = True
        self.streams[eng].append((fn, deps, tok))
        return tok

    def emit(self, nc, eng_sems, dma_sems, cc_sem):
        for e in self.ENGS:
            cnt = 0
            for fn, deps, tok in self.streams[e]:
                if tok.kind == "c" and tok.need:
                    cnt += 1
                    tok.val = cnt

        def semval(t):
            if t.kind == "d":
                return dma_sems[t.sem], t.val
            if t.kind == "cc":
                return cc_sem[t.sem], t.val
            return eng_sems[t.eng], t.val

        def run(e, eo):
            seen = {}
            for fn, deps, tok in self.streams[e]:
                for d in deps:
                    sem, val = semval(d)
                    key = id(sem)
                    if seen.get(key, 0) >= val:
                        continue
                    eo.wait_ge(sem, val)
                    seen[key] = val
                if fn is None:
                    continue
                ins = fn(eo)
                if tok.kind == "d":
                    ins.then_inc(dma_sems[tok.sem], 16)
                elif tok.kind == "cc":
                    ins.then_inc(cc_sem[tok.sem], 1)
                elif tok.val is not None:
                    ins.then_inc(eng_sems[e], 1)

        with nc.Block() as block:
            @block.sync
            def _(eo):
                run("sp", eo)

            @block.gpsimd
            def _(eo):
                run("pool", eo)

            @block.scalar
            def _(eo):
                run("act", eo)

            @block.vector
            def _(eo):
                run("dve", eo)

            @block.tensor
            def _(eo):
                run("pe", eo)


CPN = 1088


def build_nc(NL=L, stage=99):
    nc = bass.Bass("TRN2", target_bir_lowering=False)
    dt = nc.dram_tensor

    def din(name, shape):
        return dt(name, list(shape), F32, kind="ExternalInput").ap()

    def dout(name, shape):
        return dt(name, list(shape), F32, kind="ExternalOutput").ap()

    xp = din("xp", [T, D])
    xs = din("xs", [1, D])
    stc = din("stc", [NL, 2, 1024])
    ck = [din("ck128", [NL, 128, 1024]), din("ck512", [NL, 512, 1024]), din("ck2048", [NL, 2048, 1024])]
    w_in = din("w_in", [NL, D, 15872])
    w_a_out = din("w_a_out", [NL, 1024, D])
    w_b_out = din("w_b_out", [NL, 1024, D])
    w_c_out = din("w_c_out", [NL, 512, D])
    w_o = din("w_o", [NL, D, D])
    w_f1 = din("w_ffn_in", [NL, D, 2 * DFF])
    w_f2 = din("w_ffn_out", [NL, DFF, D])
    w_s = din("w_s", [NL, 4, 128, 128])
    pp_d = din("pp", [128, L * PPN])
    bsr_d = din("bsr", [128, L * 512])
    cst_d = din("cpack", [128, CPN])
    y_o = dout("y", [TT, D])
    kvout = dout("kvout", [L, TT, 3, 1024])
    cst_o = dout("cst", [L, 4, 1024])
    vch_o = dout("vch", [L, 1, 1024])
    xpark = dt("xpark", [128, 16 * TP], F32).ap().rearrange("p (c t) -> p c t", t=TP)
    CHR = [512, 512, 512, 128, 16]
    snd = [[dt("snd%d_%d" % (l, k), [CHR[k], 1024], F32).ap() for k in range(5)] for l in range(L)]
    rcv = [[dt("rcv%d_%d" % (l, k), [2 * CHR[k], 1024], F32).ap() for k in range(5)] for l in range(L)]

    P = Prog()
    pinned = set()

    def scratch(pin=False):
        while True:
            i = P.sc_i % NSC
            P.sc_i += 1
            if i not in pinned:
                break
        if pin:
            pinned.add(i)
        return i

    A0 = 0
    B0 = 16 * TP
    C0 = B0 + 8 * TP
    CW = 11 * TP + 4
    WS0 = C0 + CW
    WB0 = WS0 + 2 * 2048
    SC0 = WB0 + 3 * 1024
    NF = SC0 + NSC * SCW
    ar = nc.alloc_sbuf_tensor("arena", [128, NF], F32)
    R = ar[:, A0:A0 + 16 * TP].rearrange("p (c t) -> p c t", t=TP)
    Abf = ar[:, A0:A0 + 16 * TP].bitcast(BF16).rearrange("p (c t) -> p c t", t=TP)
    qT = Abf[:, 0:12, :]
    ya = Abf[:, 12:20, :]
    yb = Abf[:, 20:28, :]
    UZ = R[:, 14:16, :]
    vn_tm = ar[:, A0:A0 + 9 * 512].bitcast(BF16).rearrange("p (t f) -> p t f", f=1024)
    hT = ar[:, B0:B0 + 8 * TP].bitcast(BF16).rearrange("p (c t) -> p c t", t=TP)
    Cbf = ar[:, C0:C0 + 11 * TP].bitcast(BF16).rearrange("p (c t) -> p c t", t=TP)
    mT = Cbf[:, 0:16, :]
    yc = Cbf[:, 16:20, :]
    hid = Cbf
    Cf = ar[:, C0:C0 + 11 * TP].rearrange("p (c t) -> p c t", t=TP)
    wbf = [ar[:, WS0 + i * 1024:WS0 + (i + 1) * 1024].bitcast(BF16).rearrange("p (k n) -> p k n", n=128) for i in range(NWB)]

    def scf(i):
        return ar[:, SC0 + i * SCW:SC0 + (i + 1) * SCW]

    def scb(i):
        return ar[:, SC0 + i * SCW:SC0 + (i + 1) * SCW].bitcast(BF16)

    def rA(c):
        return ("A", c)

    def rQ(hh):
        return ("A", hh // 2)

    def rYA(c):
        return ("A", 6 + c // 2)

    def rYB(c):
        return ("A", 10 + c // 2)

    RUZ = [("A", 14), ("A", 15)]
    RVN = [("A", i) for i in range(5)]

    def rH(c):
        return ("B", c)

    def rM(mc):
        return ("C", mc // 2)

    def rCf(c):
        return ("C", c)

    def rYC(h):
        return ("C", 8 + h // 2)

    def rHid(j):
        return ("C", j // 2)

    def rS(i):
        return ("SC", i)

    cons = nc.alloc_sbuf_tensor("cons", [128, CPN], F32)
    ident_f = cons[:, 0:128]
    triu = cons[:, 128:256]
    maskf = cons[:, 256:768]
    cos_t = cons[:, 768:912].rearrange("p (t j) -> p t j", j=16)
    sin_t = cons[:, 912:1056].rearrange("p (t j) -> p t j", j=16)
    flag = cons[:, 1056:1057]
    epsc = cons[:, 1057:1058]
    pp = nc.alloc_sbuf_tensor("ppar", [128, L * PPN], F32)
    bsb = nc.alloc_sbuf_tensor("bsb", [128, 512], F32)
    cb = nc.alloc_sbuf_tensor("cb", [128, 768], BF16)
    ident_b = cb[:, 0:128]
    ones_b = cb[:, 128:256]
    maskb = cb[:, 256:768]
    wsT = nc.alloc_sbuf_tensor("wsT", [128, 512], BF16)
    small = nc.alloc_sbuf_tensor("small", [128, 32 * 6], F32)

    def sm3(i):
        return small[:, i * 32:(i + 1) * 32].rearrange("p (c j) -> p c j", j=4)

    abh, yah, zt, hist, stT, vcs3 = [sm3(i) for i in range(6)]

    MS = [nc.alloc_psum_tensor("MS0", [128, 1536], F32), nc.alloc_psum_tensor("MS1", [128, 1536], F32)]
    PX = nc.alloc_psum_tensor("PX", [128, 512], F32)
    PY = nc.alloc_psum_tensor("PY", [128, 512], F32)
    PXY = [PX, PY]
    RPX = [("PX",), ("PY",)]

    def bk(i, b):
        return ("bk", i, b)

    def bf(ap):
        return ap.bitcast(BF16)

    def pv(i):
        return pp[:, i:i + 1]

    def edma(sb3, j0, drows, load, **kw):
        for r, dr in enumerate(drows):
            d2 = dr.rearrange("(c p) -> p c", p=128)
            sb2 = sb3[:, :, j0 + r]
            if load:
                P.op("sp", lambda e, d2=d2, sb2=sb2: e.dma_start(out=sb2, in_=d2, allow_slow_non_contiguous=True), kind="d", **kw)
            else:
                P.op("sp", lambda e, d2=d2, sb2=sb2: e.dma_start(out=d2, in_=sb2, allow_slow_non_contiguous=True), kind="d", **kw)

    P.op("sp", lambda e: e.dma_start(out=cons[:, :], in_=cst_d[:, :]), wr=["cons"], kind="d")
    P.op("sp", lambda e: e.dma_start(out=pp[:, :], in_=pp_d[:, :]), wr=["pp"], kind="d")
    P.op("pool", lambda e: e.memset(ones_b, 1.0), wr=["ones"])
    P.op("pool", lambda e: e.tensor_copy(out=ident_b, in_=ident_f), rd=["cons"], wr=["identb"])
    P.op("pool", lambda e: e.tensor_copy(out=maskb, in_=maskf), rd=["cons"], wr=["maskb"])
    P.op("pool", lambda e: e.memset(ar[:, SC0:NF], 0.0), wr=[rS(i) for i in range(NSC)])
    P.op("pool", lambda e: e.memset(small[:, :], 0.0), wr=["small"])

    wcnt = [0]
    pcnt = [0]

    def slab(wap, k0, kc, col0):
        i = wcnt[0]
        wcnt[0] += 1
        b = i % NWB
        src = wap[k0 * 128:(k0 + kc) * 128, col0:col0 + 128].rearrange("(k p) n -> p k n", p=128)
        P.op("pool", lambda e: e.dma_start(out=wbf[b][:, 0:kc, :], in_=src), wr=[("wbf", b)], kind="d", dsem=b)
        return b

    pend = [None]

    def flush_pend():
        if pend[0] is not None:
            f_ = pend[0]
            pend[0] = None
            f_()

    def mm_chunk(parts, epi):
        i = pcnt[0] % 2
        pcnt[0] += 1
        ps = MS[i]
        nk = sum(p[2] for p in parts)
        kk = 0
        for (wap, k0, kc, col0, act, areg, a0) in parts:
            b = slab(wap, k0, kc, col0)
            for j in range(kc):
                for ti, (c0, n) in enumerate(TILES):
                    out = ps[:, POFF[ti]:POFF[ti] + n]
                    rhs = act[:, k0 + a0 + j, c0:c0 + n]
                    P.op("pe", lambda e, out=out, b=b, j=j, rhs=rhs, st=(kk == 0), sp_=(kk == nk - 1):
                         e.matmul(out=out, lhsT=wbf[b][:, j, :], rhs=rhs, start=st, stop=sp_),
                         rd=[("wbf", b), areg(k0 + a0 + j)], wr=[bk(i, ti)])
                kk += 1
        flush_pend()
        r_ = epi(ps, i)
        if callable(r_):
            pend[0] = r_

    def hpart(wap, col0):
        return [(wap, 0, 16, col0, hT, rH, 0)]

    def evac2(eng, fn2, ps, i, rd, wr):
        for ti, (c0, n) in enumerate(TILES):
            P.op(eng, lambda e, c0=c0, n=n, ti=ti: fn2(e, c0, n, ps[:, POFF[ti]:POFF[ti] + n]), rd=rd, wr=[bk(i, ti)] + wr)

    def stats_rstd(src, sreg, nch, inv_n):
        i = pcnt[0] % 2
        pcnt[0] += 1
        ps = MS[i]
        for c in range(nch):
            s = scratch()
            P.op("act", lambda e, s=s, c=c: e.activation(out=scb(s)[:, 0:TT], in_=src(c), func=AF.Square), rd=[sreg(c)], wr=[rS(s)])
            for ti, (c0, n) in enumerate(TILES):
                P.op("pe", lambda e, s=s, c=c, c0=c0, n=n, ti=ti: e.matmul(out=ps[:, POFF[ti]:POFF[ti] + n], lhsT=ones_b, rhs=scb(s)[:, c0:c0 + n], start=(c == 0), stop=(c == nch - 1)),
                     rd=[rS(s), "ones"], wr=[bk(i, ti)])
        rs = scratch(pin=True)
        for ti, (c0, n) in enumerate(TILES):
            P.op("act", lambda e, c0=c0, n=n, ti=ti: e.activation(out=scf(rs)[:, c0:c0 + n], in_=ps[:, POFF[ti]:POFF[ti] + n], func=AF.Sqrt, scale=inv_n, bias=epsc),
                 rd=["cons"], wr=[bk(i, ti), rS(rs)])
        P.op("dve", lambda e: e.reciprocal(out=scf(rs)[:, 0:TT], in_=scf(rs)[:, 0:TT]), wr=[rS(rs)])
        return rs

    def prenorm(l, gi):
        rs = stats_rstd(lambda c: R[:, c, 0:TT], rA, 16, 1.0 / D)
        for c in range(16):
            P.op("dve", lambda e, c=c: e.scalar_tensor_tensor(out=hT[:, c, 0:TT], in0=R[:, c, 0:TT], scalar=pv(l * PPN + PP_G + gi * 16 + c), in1=scf(rs)[:, 0:TT], op0=ALU.mult, op1=ALU.mult),
                 rd=[rA(c), rS(rs), "pp"], wr=[rH(c)])
        pinned.discard(rs)
        P.op("sp", lambda e: e.dma_start(out=xpark[:, :, 0:TT], in_=R[:, :, 0:TT]), rd=[rA(c) for c in range(16)], wr=["xpark"], kind="d")

    def postnorm(l, gi):
        rs = stats_rstd(lambda c: R[:, c, 0:TT], rA, 16, 1.0 / D)
        for c in range(16):
            xi = scratch()
            P.op("sp", lambda e, xi=xi, c=c: e.dma_start(out=scf(xi)[:, 0:TT], in_=xpark[:, c, 0:TT]), rd=["xpark"], wr=[rS(xi)], kind="d")
            P.op("dve", lambda e, c=c: e.scalar_tensor_tensor(out=R[:, c, 0:TT], in0=R[:, c, 0:TT], scalar=pv(l * PPN + PP_G + gi * 16 + c), in1=scf(rs)[:, 0:TT], op0=ALU.mult, op1=ALU.mult),
                 rd=[rS(rs), "pp"], wr=[rA(c)])
            P.op("dve", lambda e, xi=xi, c=c: e.tensor_tensor(out=R[:, c, 0:TT], in0=R[:, c, 0:TT], in1=scf(xi)[:, 0:TT], op=ALU.add), rd=[rS(xi)], wr=[rA(c)])
        pinned.discard(rs)

    def epi_to_R(mc):
        def epi(ps, i):
            evac2("act", lambda e, c0, n, p: e.activation(out=R[:, mc, c0:c0 + n], in_=p, func=AF.Copy), ps, i, [], [rA(mc)])
        return epi

    for t in range(9):
        bi = t % 2
        buf = ar[:, C0 + bi * 2048:C0 + (bi + 1) * 2048]
        rg = [("C", 2 * bi), ("C", 2 * bi + 1)]
        nrow = 128 if t < 8 else 1
        srcx = xp[t * 128:(t + 1) * 128, :] if t < 8 else xs[0:1, :]
        P.op("sp", lambda e, buf=buf, nrow=nrow, srcx=srcx: e.dma_start(out=buf[0:nrow, :], in_=srcx), wr=rg, kind="d")
        for c4 in range(4):
            px = PXY[c4 % 2]
            for cc in range(4):
                c = c4 * 4 + cc
                P.op("pe", lambda e, px=px, cc=cc, c=c, buf=buf, nrow=nrow: e.transpose(out=px[:, cc * 128:cc * 128 + nrow], in_=buf[0:nrow, c * 128:(c + 1) * 128], identity=ident_f[0:nrow, 0:nrow]),
                     rd=rg + ["cons"], wr=[RPX[c4 % 2]])
            P.op("act", lambda e, px=px, c4=c4, t=t, nrow=nrow: e.activation(out=R[:, c4 * 4:(c4 + 1) * 4, t * 128:t * 128 + nrow], in_=px[:, :].rearrange("p (c n) -> p c n", n=128)[:, :, 0:nrow], func=AF.Copy),
                 wr=[RPX[c4 % 2]] + [rA(c) for c in range(c4 * 4, c4 * 4 + 4)])

    isq = 1.0 / (128.0 ** 0.5)

    def layer(l):
        pl = l * PPN
        Wl = w_in[l]
        P.op("sp", lambda e, l=l: e.dma_start(out=bsb[:, :], in_=bsr_d[:, l * 512:(l + 1) * 512]), wr=["bsb"], kind="d")
        wn = scratch()
        P.op("sp", lambda e, l=l, wn=wn: e.dma_start(out=scf(wn)[:, 0:512].rearrange("p (g j) -> p g j", j=128), in_=w_s[l].rearrange("g i j -> i g j")), wr=[rS(wn)], kind="d")
        for g in range(4):
            P.op("pe", lambda e, g=g, wn=wn: e.transpose(out=PX[:, g * 128:(g + 1) * 128], in_=scf(wn)[:, g * 128:(g + 1) * 128], identity=ident_f), rd=[rS(wn), "cons"], wr=[RPX[0]])
        for g in range(4):
            P.op("dve", lambda e, g=g: e.tensor_tensor(out=wsT[:, g * 128:(g + 1) * 128], in0=PX[:, g * 128:(g + 1) * 128], in1=triu, op=ALU.mult), rd=["cons"], wr=[RPX[0], "wsT"])
        edma(stT, 0, [stc[l, 0, :], stc[l, 1, :]], True, aw=["stT"])

        prenorm(l, 0)
        if stage < 1:
            return

        for c in range(8):
            def epi_bv(ps, i, c=c):
                evac2("act", lambda e, c0, n, p: e.activation(out=Cf[:, c, c0:c0 + n], in_=p, func=AF.Copy), ps, i, [], [rCf(c)])
            mm_chunk(hpart(Wl, O_BV + c * 128), epi_bv)
        for c in range(8):
            s1 = scratch()
            P.op("act", lambda e, s1=s1, c=c: e.activation(out=scb(s1)[:, 0:TT], in_=Cf[:, c, 0:TT], func=AF.Copy), rd=[rCf(c)], wr=[rS(s1)])
            s2 = scratch()
            P.op("act", lambda e, s2=s2, c=c: e.activation(out=scb(s2)[:, 0:TT], in_=Cf[:, c, 0:TT], func=AF.Square), rd=[rCf(c)], wr=[rS(s2)])
            for ti, (c0, n) in enumerate(TILES):
                P.op("pe", lambda e, s1=s1, c=c, c0=c0, n=n, ti=ti: e.matmul(out=MS[0][:, POFF[ti]:POFF[ti] + n], lhsT=ones_b, rhs=scb(s1)[:, c0:c0 + n], start=(c == 0), stop=(c == 7)), rd=[rS(s1), "ones"], wr=[bk(0, ti)])
                P.op("pe", lambda e, s2=s2, c=c, c0=c0, n=n, ti=ti: e.matmul(out=MS[1][:, POFF[ti]:POFF[ti] + n], lhsT=ones_b, rhs=scb(s2)[:, c0:c0 + n], start=(c == 0), stop=(c == 7)), rd=[rS(s2), "ones"], wr=[bk(1, ti)])
        mu = scratch(pin=True)
        va = scratch(pin=True)
        for ti, (c0, n) in enumerate(TILES):
            P.op("act", lambda e, c0=c0, n=n, ti=ti: e.activation(out=scf(mu)[:, c0:c0 + n], in_=MS[0][:, POFF[ti]:POFF[ti] + n], func=AF.Copy, scale=1.0 / 1024), wr=[bk(0, ti), rS(mu)])
            P.op("act", lambda e, c0=c0, n=n, ti=ti: e.activation(out=scf(va)[:, c0:c0 + n], in_=MS[1][:, POFF[ti]:POFF[ti] + n], func=AF.Copy, scale=1.0 / 1024), wr=[bk(1, ti), rS(va)])
        tq = scratch()
        P.op("dve", lambda e: e.tensor_tensor(out=scf(tq)[:, 0:TT], in0=scf(mu)[:, 0:TT], in1=scf(mu)[:, 0:TT], op=ALU.mult), rd=[rS(mu)], wr=[rS(tq)])
        P.op("dve", lambda e: e.tensor_tensor(out=scf(va)[:, 0:TT], in0=scf(va)[:, 0:TT], in1=scf(tq)[:, 0:TT], op=ALU.subtract), rd=[rS(tq)], wr=[rS(va)])
        P.op("act", lambda e: e.activation(out=scf(va)[:, 0:TT], in_=scf(va)[:, 0:TT], func=AF.Sqrt, bias=epsc), rd=["cons"], wr=[rS(va)])
        P.op("dve", lambda e: e.reciprocal(out=scf(va)[:, 0:TT], in_=scf(va)[:, 0:TT]), wr=[rS(va)])
        for c in range(8):
            vf = scratch()
            P.op("dve", lambda e, vf=vf, c=c: e.tensor_tensor(out=scf(vf)[:, 0:TT], in0=Cf[:, c, 0:TT], in1=scf(mu)[:, 0:TT], op=ALU.subtract), rd=[rCf(c), rS(mu)], wr=[rS(vf)])
            P.op("dve", lambda e, vf=vf: e.tensor_tensor(out=scf(vf)[:, 0:TT], in0=scf(vf)[:, 0:TT], in1=scf(va)[:, 0:TT], op=ALU.mult), rd=[rS(va)], wr=[rS(vf)])
            P.op("dve", lambda e, vf=vf, c=c: e.tensor_scalar(out=scf(vf)[:, 0:TT], in0=scf(vf)[:, 0:TT], scalar1=pv(pl + PP_LNG + c), scalar2=pv(pl + PP_LNB + c), op0=ALU.mult, op1=ALU.add), rd=["pp"], wr=[rS(vf)])
            P.op("dve", lambda e, vf=vf, c=c: e.tensor_copy(out=vcs3[:, c, 0:1], in_=scf(vf)[:, 1024:1025]), rd=[rS(vf)], wr=["vcs"])
            P.op("dve", lambda e, vf=vf, c=c: e.tensor_scalar(out=Cf[:, c, 1024:1025], in0=scf(vf)[:, 1024:1025], scalar1=pv(pl + PP_WS0 + c), scalar2=pv(pl + PP_BS0 + c), op0=ALU.mult, op1=ALU.add), rd=[rS(vf), "pp"], wr=[rCf(c)])
            vb = scratch()
            P.op("act", lambda e, vf=vf, vb=vb: e.activation(out=scb(vb)[:, 0:1024], in_=scf(vf)[:, 0:1024], func=AF.Copy), rd=[rS(vf)], wr=[rS(vb)])
            pxb = bf(PX[:, :])
            for t in range(8):
                P.op("pe", lambda e, vb=vb, t=t, pxb=pxb: e.transpose(out=pxb[:, t * 128:(t + 1) * 128], in_=scb(vb)[:, t * 128:(t + 1) * 128], identity=ident_b), rd=[rS(vb), "identb"], wr=[RPX[0]])
            P.op("act", lambda e, c=c, pxb=pxb: e.activation(out=vn_tm[:, 0:8, c * 128:(c + 1) * 128], in_=pxb.rearrange("p (t f) -> p t f", f=128), func=AF.Copy), wr=[RPX[0]] + RVN)
        pinned.discard(mu)
        pinned.discard(va)
        edma(vcs3, 0, [vch_o[l, 0, :]], False, rd=["vcs"], aw=[("vch", l)])
        for fc in range(8):
            g = fc // 2
            for hf in range(2):
                px = PXY[hf]
                for tt in range(4):
                    t = hf * 4 + tt
                    P.op("pe", lambda e, px=px, tt=tt, t=t, fc=fc, g=g: e.matmul(out=px[:, tt * 128:(tt + 1) * 128], lhsT=vn_tm[:, t, fc * 128:(fc + 1) * 128], rhs=wsT[:, g * 128:(g + 1) * 128], start=True, stop=True),
                         rd=RVN + ["wsT"], wr=[RPX[hf]])
                for tt in range(4):
                    P.op("dve", lambda e, px=px, tt=tt, hf=hf, fc=fc, g=g: e.tensor_tensor(out=Cf[:, fc, hf * 512 + tt * 128:hf * 512 + (tt + 1) * 128], in0=px[:, tt * 128:(tt + 1) * 128], in1=bsb[:, g * 128:(g + 1) * 128], op=ALU.add),
                         rd=["bsb"], wr=[RPX[hf], rCf(fc)])
        for c in range(8):
            def epi_bu(ps, i, c=c):
                evac2("dve", lambda e, c0, n, p: e.tensor_tensor(out=yb[:, c, c0:c0 + n], in0=p, in1=Cf[:, c, c0:c0 + n], op=ALU.mult), ps, i, [rCf(c)], [rYB(c)])
            mm_chunk(hpart(Wl, O_BU + c * 128), epi_bu)

        if stage < 2:
            return
        for mc in range(36):
            typ, hh = mc // 12, mc % 12
            g, h = hh // 4, hh % 4

            def epi_qkv(ps, i, typ=typ, hh=hh, g=g, h=h, l=l):
                fm = scratch(pin=True)
                evac2("act", lambda e, c0, n, p: e.activation(out=scf(fm)[:, c0:c0 + n], in_=p, func=AF.Copy), ps, i, [], [rS(fm)])
                return lambda: post_qkv(ps, i, fm, typ, hh, g, h, l)

            def post_qkv(ps, i, fm, typ, hh, g, h, l):
                for t in range(8):
                    px = PXY[t // 4]
                    P.op("pe", lambda e, px=px, t=t: e.transpose(out=px[:, (t % 4) * 128:(t % 4 + 1) * 128], in_=scf(fm)[:, t * 128:(t + 1) * 128], identity=ident_f), rd=[rS(fm), "cons"], wr=[RPX[t // 4]])
                P.op("pe", lambda e: e.transpose(out=ps[0:1, 1400:1528], in_=scf(fm)[:, 1024:1025], identity=ident_f), rd=[rS(fm), "cons"], wr=[bk(i, 2)])
                tm = scratch()
                tm3 = scf(tm).rearrange("p (t d) -> p t d", d=128)
                for hf in range(2):
                    P.op("act", lambda e, hf=hf: e.activation(out=tm3[:, hf * 4:(hf + 1) * 4, :], in_=PXY[hf][:, :].rearrange("p (t d) -> p t d", d=128), func=AF.Copy), wr=[RPX[hf], rS(tm)])
                P.op("act", lambda e: e.activation(out=tm3[0:1, 8, :], in_=ps[0:1, 1400:1528], func=AF.Copy), wr=[bk(i, 2), rS(tm)])
                if typ < 2:
                    tp = scratch()
                    t4 = scf(tp)[:, 0:576].rearrange("p (a t j) -> p a t j", a=4, j=16)
                    x1, x2 = tm3[:, :, 0:16], tm3[:, :, 16:32]
                    for a, (xa, tb) in enumerate([(x1, cos_t), (x2, sin_t), (x2, cos_t), (x1, sin_t)]):
                        P.op("dve", lambda e, a=a, xa=xa, tb=tb: e.tensor_tensor(out=t4[:, a, :, :], in0=xa, in1=tb, op=ALU.mult), rd=[rS(tm), "cons"], wr=[rS(tp)])
                    P.op("dve", lambda e: e.tensor_tensor(out=x1, in0=t4[:, 0, :, :], in1=t4[:, 1, :, :], op=ALU.subtract), rd=[rS(tp)], wr=[rS(tm)])
                    P.op("dve", lambda e: e.tensor_tensor(out=x2, in0=t4[:, 2, :, :], in1=t4[:, 3, :, :], op=ALU.add), rd=[rS(tp)], wr=[rS(tm)])
                if typ >= 1:
                    off = (typ - 1) * 512 + h * 128
                    P.op("sp", lambda e: e.dma_start(out=kvout[l, 0:1024, g, off:off + 128].rearrange("(t p) d -> p t d", p=128), in_=tm3[:, 0:8, :]), rd=[rS(tm)], aw=[("kvout", l)], kind="d")
                    P.op("sp", lambda e: e.dma_start(out=kvout[l, 1024:1025, g, off:off + 128], in_=tm3[0:1, 8, :]), rd=[rS(tm)], aw=[("kvout", l)], kind="d")
                    for (kch, ta, tb2) in [[(3, 7, 8)], [(2, 4, 8)], [(0, 0, 4), (1, 4, 8)]][g]:
                        P.op("sp", lambda e, kch=kch, ta=ta, tb2=tb2: e.dma_start(out=snd[l][kch][0:(tb2 - ta) * 128, off:off + 128].rearrange("(t p) d -> p t d", p=128), in_=tm3[:, ta:tb2, :]), rd=[rS(tm)], aw=[("snd", l, kch)], kind="d")
                else:
                    tb_ = scratch()
                    P.op("dve", lambda e: e.tensor_copy(out=scb(tb_)[:, 0:1152], in_=scf(tm)[:, 0:1152]), rd=[rS(tm)], wr=[rS(tb_)])
                    pxb = bf(PX[:, :])
                    for t in range(8):
                        P.op("pe", lambda e, t=t: e.transpose(out=pxb[:, t * 128:(t + 1) * 128], in_=scb(tb_)[:, t * 128:(t + 1) * 128], identity=ident_b), rd=[rS(tb_), "identb"], wr=[RPX[0]])
                    P.op("act", lambda e: e.activation(out=qT[:, hh, 0:1024], in_=pxb, func=AF.Copy), wr=[RPX[0], rQ(hh)])
                    P.op("pe", lambda e: e.transpose(out=ps[:, 1380:1381], in_=tm3[0:1, 8, :], identity=ident_f[0:1, 0:1]), rd=[rS(tm), "cons"], wr=[bk(i, 2)])
                    P.op("act", lambda e: e.activation(out=qT[:, hh, 1024:1025], in_=ps[:, 1380:1381], func=AF.Copy), wr=[bk(i, 2), rQ(hh)])
                pinned.discard(fm)
            mm_chunk(hpart(Wl, O_Q + mc * 128), epi_qkv)
        flush_pend()

        for kch in range(4):
            P.op("pool", lambda e, l=l, kch=kch: e.collective_compute("AllGather", ALU.bypass, replica_groups=[[0, 1], [2, 3], [4, 5], [6, 7]], ins=[snd[l][kch].opt()], outs=[rcv[l][kch].opt()]),
                 rd=[("snd", l, kch)], wr=[("rcv", l, kch)], kind="cc", dsem=kch)
        if stage < 3:
            return
        w0c, w1c, w2c = pl + PP_CW, pl + PP_CW + 8, pl + PP_CW + 16
        for c in range(8):
            act_s = scratch(pin=True)
            zc = scratch(pin=True)
            cv = scratch(pin=True)

            def epi_ac(ps, i, act_s=act_s):
                evac2("act", lambda e, c0, n, p: e.activation(out=scf(act_s)[:, c0:c0 + n], in_=p, func=AF.Copy), ps, i, [], [rS(act_s)])

            def epi_ax(ps, i, act_s=act_s, zc=zc, c=c, cv=cv):
                P.op("dve", lambda e: e.memset(scf(zc)[:, 0:2], 0.0), wr=[rS(zc)])
                P.op("dve", lambda e: e.tensor_copy(out=scf(zc)[:, 1026:1028], in_=stT[:, c, 0:2]), rd=["stT"], wr=[rS(zc)])
                for (ti, c0, n, po, o0) in [(0, 0, 342, 0, 2), (1, 342, 342, 512, 344), (2, 684, 340, 1024, 686), (2, 1024, 1, 1364, 1028)]:
                    P.op("dve", lambda e, c0=c0, n=n, o0=o0, po=po: e.tensor_tensor(out=scf(zc)[:, o0:o0 + n], in0=ps[:, po:po + n], in1=scf(act_s)[:, c0:c0 + n], op=ALU.mult), rd=[rS(act_s)], wr=[bk(i, ti), rS(zc)])
                P.op("dve", lambda e: e.tensor_copy(out=zt[:, c, 0:2], in_=scf(zc)[:, 1024:1026]), rd=[rS(zc)], wr=["zt"])
                P.op("dve", lambda e: e.tensor_copy(out=zt[:, c, 2:4], in_=scf(zc)[:, 1027:1029]), rd=[rS(zc)], wr=["zt"])
                P.op("dve", lambda e: e.tensor_scalar(out=scf(cv)[:, 0:1027], in0=scf(zc)[:, 0:1027], scalar1=pv(w0c + c), scalar2=None, op0=ALU.mult), rd=[rS(zc), "pp"], wr=[rS(cv)])
                P.op("dve", lambda e: e.scalar_tensor_tensor(out=scf(cv)[:, 0:1027], in0=scf(zc)[:, 1:1028], scalar=pv(w1c + c), in1=scf(cv)[:, 0:1027], op0=ALU.mult, op1=ALU.add), rd=[rS(zc)], wr=[rS(cv)])
                P.op("dve", lambda e: e.scalar_tensor_tensor(out=scf(cv)[:, 0:1027], in0=scf(zc)[:, 2:1029], scalar=pv(w2c + c), in1=scf(cv)[:, 0:1027], op0=ALU.mult, op1=ALU.add), rd=[rS(zc)], wr=[rS(cv)])

            def epi_ab(ps, i, zc=zc, c=c, cv=cv):
                for (ti, c0, n, po, o0) in [(0, 0, 342, 0, 0), (1, 342, 342, 512, 342), (2, 684, 340, 1024, 684), (2, 1024, 1, 1364, 1026)]:
                    P.op("dve", lambda e, c0=c0, n=n, o0=o0, po=po: e.tensor_tensor(out=ya[:, c, c0:c0 + n], in0=ps[:, po:po + n], in1=scf(cv)[:, o0:o0 + n], op=ALU.mult), rd=[rS(cv)], wr=[bk(i, ti), rYA(c)])
                P.op("dve", lambda e: e.tensor_copy(out=abh[:, c, 0:2], in_=ps[:, 0:2]), wr=[bk(i, 0), "abh"])
                P.op("dve", lambda e: e.tensor_tensor(out=yah[:, c, 0:2], in0=ps[:, 0:2], in1=scf(cv)[:, 0:2], op=ALU.mult), rd=[rS(cv)], wr=[bk(i, 0), "yah"])
            mm_chunk(hpart(Wl, O_AC + c * 128), epi_ac)
            mm_chunk(hpart(Wl, O_AX + c * 128), epi_ax)
            mm_chunk(hpart(Wl, O_AB + c * 128), epi_ab)
            pinned.discard(act_s)
            pinned.discard(zc)
            pinned.discard(cv)
        edma(zt, 0, [snd[l][4][0, :], snd[l][4][1, :]], False, rd=["zt"], aw=[("snd", l, 4)])
        edma(zt, 0, [cst_o[l, r_, :] for r_ in range(4)], False, rd=["zt"], aw=[("cst", l)])

        if stage < 4:
            return
        P.op("pool", lambda e, l=l: e.collective_compute("AllGather", ALU.bypass, replica_groups=[[0, 1], [2, 3], [4, 5], [6, 7]], ins=[snd[l][4].opt()], outs=[rcv[l][4].opt()]),
             rd=[("snd", l, 4)], wr=[("rcv", l, 4)], kind="cc", dsem=4)
        if stage < 5:
            return
        def own(g_, r0, step, n=128):
            return kvout[l, r0:r0 + (n - 1) * step + 1:step, g_, :], [("kvout", l)]

        def prv(kch, r0, step, n=128):
            return rcv[l][kch][r0:r0 + (n - 1) * step + 1:step, :], [("rcv", l, kch)]

        specs = []
        specs.append((0, [prv(3, 0, 1) + (0, 128)], (256, 128), (0, 1, 128)))
        for kb in range(8):
            if kb < 7:
                specs.append((0, [own(0, kb * 128, 1) + (0, 128)], (0, 256), (kb * 128, 1, 256)))
            else:
                specs.append((0, [own(0, kb * 128, 1) + (0, 128)], (0, 128), (kb * 128, 1, 128)))
        for r in range(4):
            specs.append((1, [prv(2, r, 4) + (0, 128)], (256, 128), (r, 4, 128)))
            specs.append((1, [own(1, r, 4) + (0, 128)], (0, 256), (r, 4, 256)))
            specs.append((1, [own(1, 512 + r, 4) + (0, 128)], (0, 128), (512 + r, 4, 128)))
        for r in range(16):
            specs.append((2, [prv(0, r, 16, 32) + (0, 32), prv(1, r, 16, 32) + (32, 32), own(2, r, 16, 64) + (64, 64)], (384, 64), (r, 16, 64)))
        for g_ in range(3):
            dil = [1, 4, 16][g_]
            specs.append((g_, [(ck[g_][l, 0:127 * dil + 1:dil, :], [], 0, 128)], None, (1024, 1, 1)))
            specs.append((g_, [own(g_, 1024, 1, 1) + (0, 1)], (448, 1), (1024, 1, 1)))

        steps = []
        for (g_, srcs, mk, (q0, qs, N)) in specs:
            if N == 256:
                steps.append((g_, srcs, (mk[0], 128), (q0, qs, 128)))
                steps.append((g_, srcs, (mk[0] + 128, 128), (q0 + 128 * qs, qs, 128)))
            else:
                steps.append((g_, srcs, mk, (q0, qs, N)))
        UZ4 = Cf[:, 0:8, :]
        RUZ4 = [rCf(c_) for c_ in range(8)]
        P.op("pool", lambda e: e.memset(UZ4[:, :, 0:TT], 0.0), wr=RUZ4)
        ctx = {}

        def stA(si):
            (g_, srcs, mk, (q0, qs, N)) = steps[si]
            par = si % 2
            kb_ = scratch()
            kb2 = scb(kb_)[:, 0:1024]
            kts = scb(kb_)[:, 1024:1536]
            pt = scb(kb_)[:, 1536:2048]
            for (sap, sreg, p0, pn) in srcs:
                P.op("pool", lambda e, sap=sap, p0=p0, pn=pn: e.dma_start(out=kb2[p0:p0 + pn, :], in_=sap), rd=sreg, aw=[rS(kb_)], kind="d")
            ms = MS[par]
            ktp = bf(ms[:, 1024:1536])
            for h in range(4):
                P.op("pe", lambda e, h=h: e.transpose(out=ktp[:, h * 128:(h + 1) * 128], in_=kb2[:, h * 128:(h + 1) * 128], identity=ident_b), rd=[rS(kb_), "identb"], wr=[bk(par, 2)])
            P.op("act", lambda e: e.activation(out=kts, in_=ktp[:, 0:512], func=AF.Copy), wr=[bk(par, 2), rS(kb_)])
            ctx[si] = (par, kb_, kb2, kts, pt, ms)

        def stB(si):
            (g_, srcs, mk, (q0, qs, N)) = steps[si]
            (par, kb_, kb2, kts, pt, ms) = ctx[si]
            px = PXY[par]
            for h in range(4):
                qap = qT[:, g_ * 4 + h, q0:q0 + (N - 1) * qs + 1:qs]
                P.op("pe", lambda e, qap=qap, h=h: e.matmul(out=px[:, h * N:(h + 1) * N], lhsT=kts[:, h * 128:(h + 1) * 128], rhs=qap, start=True, stop=(mk is None)), rd=[rS(kb_), rQ(g_ * 4 + h)], wr=[RPX[par]])
                if mk is not None:
                    P.op("pe", lambda e, h=h: e.matmul(out=px[:, h * N:(h + 1) * N], lhsT=ident_b, rhs=maskb[:, mk[0]:mk[0] + N], start=False, stop=True), rd=["identb", "maskb"], wr=[RPX[par]])
            P.op("act", lambda e: e.activation(out=pt[:, 0:4 * N], in_=px[:, 0:4 * N], func=AF.Exp, scale=isq), wr=[RPX[par], rS(kb_)])

        def stC(si):
            (g_, srcs, mk, (q0, qs, N)) = steps[si]
            (par, kb_, kb2, kts, pt, ms) = ctx.pop(si)
            for h in range(4):
                bkh = bk(par, h // 2)
                P.op("pe", lambda e, h=h: e.matmul(out=ms[:, (2 * h) * 128:(2 * h) * 128 + N], lhsT=kb2[:, 512 + h * 128:512 + (h + 1) * 128], rhs=pt[:, h * N:(h + 1) * N], start=True, stop=True), rd=[rS(kb_)], wr=[bkh])
                P.op("pe", lambda e, h=h: e.matmul(out=ms[:, (2 * h + 1) * 128:(2 * h + 1) * 128 + N], lhsT=ones_b, rhs=pt[:, h * N:(h + 1) * N], start=True, stop=True), rd=[rS(kb_), "ones"], wr=[bkh])
            uz = UZ4[:, :, q0:q0 + (N - 1) * qs + 1:qs]
            P.op("dve", lambda e: e.tensor_tensor(out=uz, in0=uz, in1=ms[:, 0:1024].rearrange("p (c n) -> p c n", n=128)[:, :, 0:N], op=ALU.add), wr=[bk(par, 0), bk(par, 1)] + RUZ4)

        nst = len(steps)
        for it in range(nst + 2):
            if it < nst:
                stA(it)
            if 0 <= it - 1 < nst:
                stB(it - 1)
            if 0 <= it - 2 < nst:
                stC(it - 2)
        for h in range(4):
            rc = scratch()
            P.op("dve", lambda e, rc=rc, h=h: e.reciprocal(out=scf(rc)[:, 0:TT], in_=UZ4[:, 2 * h + 1, 0:TT]), rd=RUZ4, wr=[rS(rc)])
            P.op("dve", lambda e, rc=rc, h=h: e.tensor_tensor(out=yc[:, h, 0:TT], in0=UZ4[:, 2 * h, 0:TT], in1=scf(rc)[:, 0:TT], op=ALU.mult), rd=RUZ4 + [rS(rc)], wr=[rYC(h)])

        edma(hist, 0, [rcv[l][4][0, :], rcv[l][4][1, :]], True, rd=[("rcv", l, 4)], aw=["hist"])
        P.op("dve", lambda e: e.tensor_scalar(out=hist[:, :, 0:2], in0=hist[:, :, 0:2], scalar1=flag, scalar2=None, op0=ALU.mult), rd=["cons"], wr=["hist"])
        W0, W1 = pp[:, w0c:w0c + 8], pp[:, w1c:w1c + 8]
        h0, h1 = hist[:, :, 0], hist[:, :, 1]
        t1, t2 = hist[:, :, 2], hist[:, :, 3]
        ops = [
            (t1, W0, h0, ALU.mult), (t2, W1, h1, ALU.mult), (t1, t1, t2, ALU.add), (t1, t1, abh[:, :, 0], ALU.mult), (t1, t1, yah[:, :, 0], ALU.add),
            (t2, W0, h1, ALU.mult), (t2, t2, abh[:, :, 1], ALU.mult), (t2, t2, yah[:, :, 1], ALU.add),
        ]
        for (o_, a_, b_, op_) in ops:
            P.op("dve", lambda e, o_=o_, a_=a_, b_=b_, op_=op_: e.tensor_tensor(out=o_, in0=a_, in1=b_, op=op_), rd=["pp", "abh", "yah"], wr=["hist"])
        P.op("dve", lambda e: e.tensor_copy(out=ya[:, :, 0], in_=t1), rd=["hist"], wr=[rYA(c) for c in range(8)])
        P.op("dve", lambda e: e.tensor_copy(out=ya[:, :, 1], in_=t2), rd=["hist"], wr=[rYA(c) for c in range(8)])

        if stage < 6:
            return
        for mc in range(16):
            macc = scratch(pin=True)
            for bi, (gcol, wout, kc, actT, areg) in enumerate([(O_GA, w_a_out, 8, ya, rYA), (O_GB, w_b_out, 8, yb, rYB), (O_GC, w_c_out, 4, yc, rYC)]):
                sg = scratch(pin=True)

                def epi_g(ps, i, sg=sg):
                    evac2("act", lambda e, c0, n, p: e.activation(out=scf(sg)[:, c0:c0 + n], in_=p, func=AF.Sigmoid), ps, i, [], [rS(sg)])

                def epi_o(ps, i, sg=sg, bi=bi, macc=macc, mc=mc):
                    if bi == 0:
                        evac2("dve", lambda e, c0, n, p: e.tensor_tensor(out=scf(macc)[:, c0:c0 + n], in0=p, in1=scf(sg)[:, c0:c0 + n], op=ALU.mult), ps, i, [rS(sg)], [rS(macc)])
                    else:
                        evac2("dve", lambda e, c0, n, p: e.tensor_tensor(out=scf(sg)[:, c0:c0 + n], in0=p, in1=scf(sg)[:, c0:c0 + n], op=ALU.mult), ps, i, [], [rS(sg)])
                        if bi == 1:
                            P.op("dve", lambda e: e.tensor_tensor(out=scf(macc)[:, 0:TT], in0=scf(macc)[:, 0:TT], in1=scf(sg)[:, 0:TT], op=ALU.add), rd=[rS(sg)], wr=[rS(macc)])
                        else:
                            P.op("dve", lambda e: e.tensor_tensor(out=mT[:, mc, 0:TT], in0=scf(macc)[:, 0:TT], in1=scf(sg)[:, 0:TT], op=ALU.add), rd=[rS(sg), rS(macc)], wr=[rM(mc)])
                mm_chunk(hpart(Wl, gcol + mc * 128), epi_g)
                mm_chunk([(wout[l], 0, kc, mc * 128, actT, areg, 0)], epi_o)
                pinned.discard(sg)
            pinned.discard(macc)

        if stage < 7:
            return
        for mc in range(16):
            mm_chunk([(w_o[l], 0, 16, mc * 128, mT, rM, 0)], epi_to_R(mc))
        postnorm(l, 1)

        if stage < 8:
            return
        prenorm(l, 2)
        for hf in range(2):
            for j in range(22):
                jj = hf * 22 + j
                sg = scratch(pin=True)

                def epi_gate(ps, i, sg=sg):
                    evac2("act", lambda e, c0, n, p: e.activation(out=scf(sg)[:, c0:c0 + n], in_=p, func=AF.Silu), ps, i, [], [rS(sg)])

                def epi_up(ps, i, sg=sg, j=j):
                    evac2("dve", lambda e, c0, n, p: e.tensor_tensor(out=hid[:, j, c0:c0 + n], in0=p, in1=scf(sg)[:, c0:c0 + n], op=ALU.mult), ps, i, [rS(sg)], [rHid(j)])
                mm_chunk(hpart(w_f1[l], jj * 128), epi_gate)
                mm_chunk(hpart(w_f1[l], DFF + jj * 128), epi_up)
                pinned.discard(sg)
            for mc in range(16):
                parts = [(w_f2[l], hf * 22, 11, mc * 128, hid, rHid, -hf * 22), (w_f2[l], hf * 22 + 11, 11, mc * 128, hid, rHid, -hf * 22)]
                if hf == 0:
                    mm_chunk(parts, epi_to_R(mc))
                else:
                    def epi_acc(ps, i, mc=mc):
                        evac2("dve", lambda e, c0, n, p: e.tensor_tensor(out=R[:, mc, c0:c0 + n], in0=p, in1=R[:, mc, c0:c0 + n], op=ALU.add), ps, i, [], [rA(mc)])
                    mm_chunk(parts, epi_acc)
        postnorm(l, 3)

    for l_ in range(NL):
        layer(l_)

    outs = []
    for t in range(9):
        bi = t % 2
        buf = ar[:, C0 + bi * 2048:C0 + (bi + 1) * 2048]
        rg = [("C", 2 * bi), ("C", 2 * bi + 1)]
        nrow = 128 if t < 8 else 1
        for c4 in range(4):
            px = PXY[c4 % 2]
            for cc in range(4):
                c = c4 * 4 + cc
                P.op("pe", lambda e, px=px, cc=cc, c=c, t=t, nrow=nrow: e.transpose(out=px[0:nrow, cc * 128:(cc + 1) * 128], in_=R[:, c, t * 128:t * 128 + nrow], identity=ident_f), rd=[rA(c), "cons"], wr=[RPX[c4 % 2]])
            P.op("act", lambda e, px=px, c4=c4, buf=buf, nrow=nrow: e.activation(out=buf[0:nrow, c4 * 512:(c4 + 1) * 512], in_=px[0:nrow, :], func=AF.Copy), wr=[RPX[c4 % 2]] + rg)
        outs.append(P.op("sp", lambda e, buf=buf, nrow=nrow, t=t: e.dma_start(out=y_o[t * 128:t * 128 + nrow, :], in_=buf[0:nrow, :]), rd=rg, aw=["yout"], kind="d"))
    fin = list(outs)
    for key in list(P.reg.keys()):
        if isinstance(key, tuple) and key[0] in ("kvout", "cst", "vch"):
            fin += P.reg[key][0]
    P.op("sp", None, extra=fin)

    nsem = {}
    eng_sems = {e: nc.alloc_semaphore("sem_" + e) for e in Prog.ENGS}
    dma_sems = [nc.alloc_semaphore("sem_dma%d" % i) for i in range(NDMA + NWB)]
    cc_sem = [nc.alloc_semaphore("sem_cc%d" % i) for i in range(5)]
    P.emit(nc, eng_sems, dma_sems, cc_sem)
    return nc


_NC = None


def kernel(**inputs):
    return _run(inputs, L, 99)


def _run(inputs, NL, stage):
    global _NC
    f = lambda k: np.ascontiguousarray(np.asarray(inputs[k], dtype=np.float32))
    x_prompt, x_sample = f("x_prompt"), f("x_sample")
    state_conv = f("state_conv")
    c128, c512, c2048 = f("cache_kv_w128"), f("cache_kv_w512"), f("cache_kv_w2048")
    W = {k: np.ascontiguousarray(f(k)[:NL]) for k in ["w_in", "w_a_out", "w_b_out", "w_c_out", "w_o", "w_ffn_in", "w_ffn_out", "w_s"]}
    pp = np.zeros((128, L, PPN), np.float32)
    for gi, k in enumerate(["g_pre_mix", "g_post_mix", "g_pre_ffn", "g_post_ffn"]):
        pp[:, :, PP_G + gi * 16:PP_G + (gi + 1) * 16] = f(k).reshape(L, 16, 128).transpose(2, 0, 1)
    cw = f("conv_w").reshape(L, 3, 8, 128)
    pp[:, :, PP_CW:PP_CW + 24] = cw.transpose(3, 0, 1, 2).reshape(128, L, 24)
    pp[:, :, PP_LNG:PP_LNG + 8] = f("ln_g").reshape(L, 8, 128).transpose(2, 0, 1)
    pp[:, :, PP_LNB:PP_LNB + 8] = f("ln_b").reshape(L, 8, 128).transpose(2, 0, 1)
    ws = f("w_s")
    bs = f("b_s")
    for c in range(8):
        pp[:, :, PP_WS0 + c] = ws[:, c // 2, 0, 0][None, :]
        pp[:, :, PP_BS0 + c] = bs[:, c // 2, 0][None, :]
    pp = np.ascontiguousarray(pp.reshape(128, L * PPN))
    bsr = np.ascontiguousarray(np.broadcast_to(bs.reshape(1, L * 512), (128, L * 512)))
    NEG = -30000.0
    jj = np.arange(128)[:, None]
    ii = np.arange(128)[None, :]
    diag = np.where(jj <= ii, 0.0, NEG).astype(np.float32)
    prevb = np.where(jj >= ii, 0.0, NEG).astype(np.float32)
    inv_freq = (np.float32(500000.0) ** (-np.arange(0, 32, 2, dtype=np.float32) / np.float32(32))).astype(np.float32)
    in_maps = []
    for c in range(8):
        b, half = c // 2, c % 2
        cp = np.zeros((128, CPN), np.float32)
        cp[:, 0:128] = np.eye(128, dtype=np.float32)
        cp[:, 128:256] = (jj <= ii).astype(np.float32)
        cp[:, 256:384] = diag
        cp[:, 384:512] = prevb
        cp[:, 512:640] = prevb if half == 1 else NEG
        g2 = np.where(np.arange(128)[:, None] <= 64 + np.arange(64)[None, :], 0.0, NEG).astype(np.float32)
        if half == 0:
            g2[0:64, :] = NEG
        cp[:, 640:704] = g2
        cp[:, 704] = NEG
        cp[0, 704] = 0.0
        pos = np.zeros((128, 9), np.float32)
        for t in range(8):
            pos[:, t] = half * 1024 + t * 128 + np.arange(128)
        pos[:, 8] = 16384.0
        ang = pos[:, :, None].astype(np.float32) * inv_freq[None, None, :]
        cp[:, 768:912] = np.cos(ang).astype(np.float32).reshape(128, 144)
        cp[:, 912:1056] = np.sin(ang).astype(np.float32).reshape(128, 144)
        cp[:, 1056] = float(half)
        cp[:, 1057] = EPS
        m = {
            "xp": np.ascontiguousarray(x_prompt[b, half * 1024:(half + 1) * 1024]),
            "xs": np.ascontiguousarray(x_sample[c]),
            "stc": np.ascontiguousarray(state_conv[:NL, c]),
            "ck128": np.ascontiguousarray(c128[:NL, c].reshape(NL, 128, 1024)),
            "ck512": np.ascontiguousarray(c512[:NL, c].reshape(NL, 512, 1024)),
            "ck2048": np.ascontiguousarray(c2048[:NL, c].reshape(NL, 2048, 1024)),
            "pp": pp, "bsr": bsr, "cpack": cp,
        }
        m.update(W)
        in_maps.append(m)
    if NL == L and stage == 99:
        if _NC is None:
            _NC = build_nc()
        ncx = _NC
    else:
        ncx = build_nc(NL, stage)
    res = run_bass_kernel_spmd(ncx, in_maps, core_ids=list(range(8)))
    R_ = res.results
    y_prompt = np.zeros((4, 2048, D), np.float32)
    y_sample = np.zeros((8, 1, D), np.float32)
    csp = np.zeros((L, 4, 2, 1024), np.float32)
    k128 = np.zeros((L, 4, 128, 2, 4, 128), np.float32)
    k512 = np.zeros((L, 4, 512, 2, 4, 128), np.float32)
    k2048 = np.zeros((L, 4, 2048, 2, 4, 128), np.float32)
    css = np.zeros((L, 8, 2, 1024), np.float32)
    s128 = np.zeros((L, 8, 1, 2, 4, 128), np.float32)
    s512 = np.zeros((L, 8, 1, 2, 4, 128), np.float32)
    s2048 = np.zeros((L, 8, 1, 2, 4, 128), np.float32)
    vcs = np.zeros((L, 8, 1, 1024), np.float32)
    for c in range(8):
        b, half = c // 2, c % 2
        r = R_[c]
        y_prompt[b, half * 1024:(half + 1) * 1024] = r["y"][0:1024]
        y_sample[c, 0] = r["y"][1024]
        kv = r["kvout"]
        k2048[:, b, half * 1024:(half + 1) * 1024] = kv[:, 0:1024, 2].reshape(L, 1024, 2, 4, 128)
        if half == 1:
            csp[:, b] = r["cst"][:, 0:2]
            k128[:, b] = kv[:, 896:1024, 0].reshape(L, 128, 2, 4, 128)
            k512[:, b] = kv[:, 512:1024, 1].reshape(L, 512, 2, 4, 128)
        css[:, c] = r["cst"][:, 2:4]
        s128[:, c, 0] = kv[:, 1024, 0].reshape(L, 2, 4, 128)
        s512[:, c, 0] = kv[:, 1024, 1].reshape(L, 2, 4, 128)
        s2048[:, c, 0] = kv[:, 1024, 2].reshape(L, 2, 4, 128)
        vcs[:, c, 0] = r["vch"][:, 0]
    return (y_prompt, y_sample, csp, k128, k512, k2048, css, s128, s512, s2048, vcs)
```

```python
import numpy as np
import concourse.bass as bass
import concourse.mybir as mybir
from concourse.bass_utils import run_bass_kernel_spmd

F32, BF16 = mybir.dt.float32, mybir.dt.bfloat16
AF = mybir.ActivationFunctionType
ALU = mybir.AluOpType

L = 4
D = 2048
T = 1024
TT = 1025
TP = 1028
EPS = 1e-6
O_AB, O_AC, O_AX, O_BU, O_BV, O_Q, O_K, O_V, O_GA, O_GB, O_GC = 0, 1024, 2048, 3072, 4096, 5120, 6656, 8192, 9728, 11776, 13824
DFF = 5632
TILES = [(0, 512), (512, 512), (1024, 1)]
NDMA = 24
NWB = 7
SCW = 1152
NSC = 6
PP_G = 0
PP_CW = 64
PP_LNG = 88
PP_LNB = 96
PP_WS0 = 104
PP_BS0 = 112
PPN = 120
SND_ROWS = 1666


class Tok:
    __slots__ = ("eng", "kind", "sem", "val", "need")

    def __init__(self, eng, kind):
        self.eng, self.kind, self.sem, self.val, self.need = eng, kind, None, None, False


class Prog:
    ENGS = ("pe", "act", "dve", "pool", "sp")

    def __init__(self):
        self.streams = {e: [] for e in self.ENGS}
        self.reg = {}
        self.dma_n = 0
        self.dma_last = [None] * (NDMA + NWB)
        self.dma_cnt = [0] * (NDMA + NWB)
        self.cc_cnts = [0, 0, 0, 0, 0]
        self.sc_i = 0

    def scratch(self):
        i = self.sc_i % NSC
        self.sc_i += 1
        return i

    def op(self, eng, fn, rd=(), wr=(), aw=(), kind="c", extra=(), dsem=None):
        deps = []

        def add(t):
            if t is None:
                return
            if t.eng == "pe" and eng == "pe" and t.kind == "c" and kind == "c":
                return
            deps.append(t)

        for r in rd:
            e = self.reg.get(r)
            if e:
                for t in e[0]:
                    add(t)
        for r in list(wr) + list(aw):
            e = self.reg.get(r)
            if e:
                if not (r in aw and kind == "d"):
                    for t in e[0]:
                        add(t)
                for t in e[1].values():
                    add(t)
                for t in e[2]:
                    add(t)
        for t in extra:
            add(t)
        tok = Tok(eng, kind)
        if kind == "d":
            if dsem is None:
                j = self.dma_n % NDMA
                self.dma_n += 1
            else:
                j = NDMA + dsem
            add(self.dma_last[j])
            self.dma_cnt[j] += 1
            tok.sem, tok.val = j, 16 * self.dma_cnt[j]
            self.dma_last[j] = tok
        elif kind == "cc":
            self.cc_cnts[dsem] += 1
            tok.sem, tok.val = dsem, self.cc_cnts[dsem]
        for r in rd:
            e = self.reg.setdefault(r, [[], {}, []])
            if kind == "c":
                e[1][eng] = tok
            else:
                e[2].append(tok)
        for r in wr:
            self.reg[r] = [[tok], {}, []]
        for r in aw:
            e = self.reg.get(r)
            if e and kind == "d" and not e[1] and not e[2]:
                e[0].append(tok)
            else:
                self.reg[r] = [[tok], {}, []]
        for t in deps:
            t.need = True
        self.streams[eng].append((fn, deps, tok))
        return tok

    def emit(self, nc, eng_sems, dma_sems, cc_sem):
        for e in self.ENGS:
            cnt = 0
            for fn, deps, tok in self.streams[e]:
                if tok.kind == "c" and tok.need:
                    cnt += 1
                    tok.val = cnt

        def semval(t):
            if t.kind == "d":
                return dma_sems[t.sem], t.val
            if t.kind == "cc":
                return cc_sem[t.sem], t.val
            return eng_sems[t.eng], t.val

        def run(e, eo):
            seen = {}
            for fn, deps, tok in self.streams[e]:
                for d in deps:
                    sem, val = semval(d)
                    key = id(sem)
                    if seen.get(key, 0) >= val:
                        continue
                    eo.wait_ge(sem, val)
                    seen[key] = val
                if fn is None:
                    continue
                ins = fn(eo)
                if tok.kind == "d":
                    ins.then_inc(dma_sems[tok.sem], 16)
                elif tok.kind == "cc":
                    ins.then_inc(cc_sem[tok.sem], 1)
                elif tok.val is not None:
                    ins.then_inc(eng_sems[e], 1)

        with nc.Block() as block:
            @block.sync
            def _(eo):
                run("sp", eo)

            @block.gpsimd
            def _(eo):
                run("pool", eo)

            @block.scalar
            def _(eo):
                run("act", eo)

            @block.vector
            def _(eo):
                run("dve", eo)

            @block.tensor
            def _(eo):
                run("pe", eo)


CPN = 1088


def build_nc(NL=L, stage=99):
    nc = bass.Bass("TRN2", target_bir_lowering=False)
    dt = nc.dram_tensor

    def din(name, shape):
        return dt(name, list(shape), F32, kind="ExternalInput").ap()

    def dout(name, shape):
        return dt(name, list(shape), F32, kind="ExternalOutput").ap()

    xp = din("xp", [T, D])
    xs = din("xs", [1, D])
    stc = din("stc", [NL, 2, 1024])
    ck = [din("ck128", [NL, 128, 1024]), din("ck512", [NL, 512, 1024]), din("ck2048", [NL, 2048, 1024])]
    w_in = din("w_in", [NL, D, 15872])
    w_a_out = din("w_a_out", [NL, 1024, D])
    w_b_out = din("w_b_out", [NL, 1024, D])
    w_c_out = din("w_c_out", [NL, 512, D])
    w_o = din("w_o", [NL, D, D])
    w_f1 = din("w_ffn_in", [NL, D, 2 * DFF])
    w_f2 = din("w_ffn_out", [NL, DFF, D])
    w_s = din("w_s", [NL, 4, 128, 128])
    pp_d = din("pp", [128, L * PPN])
    bsr_d = din("bsr", [128, L * 512])
    cst_d = din("cpack", [128, CPN])
    y_o = dout("y", [TT, D])
    kvout = dout("kvout", [L, TT, 3, 1024])
    cst_o = dout("cst", [L, 4, 1024])
    vch_o = dout("vch", [L, 1, 1024])
    xpark = dt("xpark", [128, 16 * TP], F32).ap().rearrange("p (c t) -> p c t", t=TP)
    CHR = [512, 512, 512, 128, 16]
    snd = [[dt("snd%d_%d" % (l, k), [CHR[k], 1024], F32).ap() for k in range(5)] for l in range(L)]
    rcv = [[dt("rcv%d_%d" % (l, k), [2 * CHR[k], 1024], F32).ap() for k in range(5)] for l in range(L)]

    P = Prog()
    pinned = set()

    def scratch(pin=False):
        while True:
            i = P.sc_i % NSC
            P.sc_i += 1
            if i not in pinned:
                break
        if pin:
            pinned.add(i)
        return i

    A0 = 0
    B0 = 16 * TP
    C0 = B0 + 8 * TP
    CW = 11 * TP + 4
    WS0 = C0 + CW
    WB0 = WS0 + 2 * 2048
    SC0 = WB0 + 3 * 1024
    NF = SC0 + NSC * SCW
    ar = nc.alloc_sbuf_tensor("arena", [128, NF], F32)
    R = ar[:, A0:A0 + 16 * TP].rearrange("p (c t) -> p c t", t=TP)
    Abf = ar[:, A0:A0 + 16 * TP].bitcast(BF16).rearrange("p (c t) -> p c t", t=TP)
    qT = Abf[:, 0:12, :]
    ya = Abf[:, 12:20, :]
    yb = Abf[:, 20:28, :]
    UZ = R[:, 14:16, :]
    vn_tm = ar[:, A0:A0 + 9 * 512].bitcast(BF16).rearrange("p (t f) -> p t f", f=1024)
    hT = ar[:, B0:B0 + 8 * TP].bitcast(BF16).rearrange("p (c t) -> p c t", t=TP)
    Cbf = ar[:, C0:C0 + 11 * TP].bitcast(BF16).rearrange("p (c t) -> p c t", t=TP)
    mT = Cbf[:, 0:16, :]
    yc = Cbf[:, 16:20, :]
    hid = Cbf
    Cf = ar[:, C0:C0 + 11 * TP].rearrange("p (c t) -> p c t", t=TP)
    wbf = [ar[:, WS0 + i * 1024:WS0 + (i + 1) * 1024].bitcast(BF16).rearrange("p (k n) -> p k n", n=128) for i in range(NWB)]

    def scf(i):
        return ar[:, SC0 + i * SCW:SC0 + (i + 1) * SCW]

    def scb(i):
        return ar[:, SC0 + i * SCW:SC0 + (i + 1) * SCW].bitcast(BF16)

    def rA(c):
        return ("A", c)

    def rQ(hh):
        return ("A", hh // 2)

    def rYA(c):
        return ("A", 6 + c // 2)

    def rYB(c):
        return ("A", 10 + c // 2)

    RUZ = [("A", 14), ("A", 15)]
    RVN = [("A", i) for i in range(5)]

    def rH(c):
        return ("B", c)

    def rM(mc):
        return ("C", mc // 2)

    def rCf(c):
        return ("C", c)

    def rYC(h):
        return ("C", 8 + h // 2)

    def rHid(j):
        return ("C", j // 2)

    def rS(i):
        return ("SC", i)

    cons = nc.alloc_sbuf_tensor("cons", [128, CPN], F32)
    ident_f = cons[:, 0:128]
    triu = cons[:, 128:256]
    maskf = cons[:, 256:768]
    cos_t = cons[:, 768:912].rearrange("p (t j) -> p t j", j=16)
    sin_t = cons[:, 912:1056].rearrange("p (t j) -> p t j", j=16)
    flag = cons[:, 1056:1057]
    epsc = cons[:, 1057:1058]
    pp = nc.alloc_sbuf_tensor("ppar", [128, L * PPN], F32)
    bsb = nc.alloc_sbuf_tensor("bsb", [128, 512], F32)
    cb = nc.alloc_sbuf_tensor("cb", [128, 768], BF16)
    ident_b = cb[:, 0:128]
    ones_b = cb[:, 128:256]
    maskb = cb[:, 256:768]
    wsT = nc.alloc_sbuf_tensor("wsT", [128, 512], BF16)
    small = nc.alloc_sbuf_tensor("small", [128, 32 * 6], F32)

    def sm3(i):
        return small[:, i * 32:(i + 1) * 32].rearrange("p (c j) -> p c j", j=4)

    abh, yah, zt, hist, stT, vcs3 = [sm3(i) for i in range(6)]

    MS = [nc.alloc_psum_tensor("MS0", [128, 1536], F32), nc.alloc_psum_tensor("MS1", [128, 1536], F32)]
    PX = nc.alloc_psum_tensor("PX", [128, 512], F32)
    PY = nc.alloc_psum_tensor("PY", [128, 512], F32)
    PXY = [PX, PY]
    RPX = [("PX",), ("PY",)]

    def bk(i, b):
        return ("bk", i, b)

    def bf(ap):
        return ap.bitcast(BF16)

    def pv(i):
        return pp[:, i:i + 1]

    def edma(sb3, j0, drows, load, **kw):
        for r, dr in enumerate(drows):
            d2 = dr.rearrange("(c p) -> p c", p=128)
            sb2 = sb3[:, :, j0 + r]
            if load:
                P.op("sp", lambda e, d2=d2, sb2=sb2: e.dma_start(out=sb2, in_=d2, allow_slow_non_contiguous=True), kind="d", **kw)
            else:
                P.op("sp", lambda e, d2=d2, sb2=sb2: e.dma_start(out=d2, in_=sb2, allow_slow_non_contiguous=True), kind="d", **kw)

    P.op("sp", lambda e: e.dma_start(out=cons[:, :], in_=cst_d[:, :]), wr=["cons"], kind="d")
    P.op("sp", lambda e: e.dma_start(out=pp[:, :], in_=pp_d[:, :]), wr=["pp"], kind="d")
    P.op("pool", lambda e: e.memset(ones_b, 1.0), wr=["ones"])
    P.op("pool", lambda e: e.tensor_copy(out=ident_b, in_=ident_f), rd=["cons"], wr=["identb"])
    P.op("pool", lambda e: e.tensor_copy(out=maskb, in_=maskf), rd=["cons"], wr=["maskb"])
    P.op("pool", lambda e: e.memset(ar[:, SC0:NF], 0.0), wr=[rS(i) for i in range(NSC)])
    P.op("pool", lambda e: e.memset(small[:, :], 0.0), wr=["small"])

    wcnt = [0]
    pcnt = [0]

    def slab(wap, k0, kc, col0):
        i = wcnt[0]
        wcnt[0] += 1
        b = i % NWB
        src = wap[k0 * 128:(k0 + kc) * 128, col0:col0 + 128].rearrange("(k p) n -> p k n", p=128)
        P.op("pool", lambda e: e.dma_start(out=wbf[b][:, 0:kc, :], in_=src), wr=[("wbf", b)], kind="d", dsem=b)
        return b

    pend = [None]

    def flush_pend():
        if pend[0] is not None:
            f_ = pend[0]
            pend[0] = None
            f_()

    def mm_chunk(parts, epi):
        i = pcnt[0] % 2
        pcnt[0] += 1
        ps = MS[i]
        nk = sum(p[2] for p in parts)
        kk = 0
        for (wap, k0, kc, col0, act, areg, a0) in parts:
            b = slab(wap, k0, kc, col0)
            for j in range(kc):
                for ti, (c0, n) in enumerate(TILES):
                    out = ps[:, c0:c0 + n]
                    rhs = act[:, k0 + a0 + j, c0:c0 + n]
                    P.op("pe", lambda e, out=out, b=b, j=j, rhs=rhs, st=(kk == 0), sp_=(kk == nk - 1):
                         e.matmul(out=out, lhsT=wbf[b][:, j, :], rhs=rhs, start=st, stop=sp_),
                         rd=[("wbf", b), areg(k0 + a0 + j)], wr=[bk(i, ti)])
                kk += 1
        flush_pend()
        r_ = epi(ps, i)
        if callable(r_):
            pend[0] = r_

    def hpart(wap, col0):
        return [(wap, 0, 16, col0, hT, rH, 0)]

    def evac2(eng, fn2, ps, i, rd, wr):
        for ti, (c0, n) in enumerate(TILES):
            P.op(eng, lambda e, c0=c0, n=n: fn2(e, c0, n, ps[:, c0:c0 + n]), rd=rd, wr=[bk(i, ti)] + wr)

    def stats_rstd(src, sreg, nch, inv_n):
        i = pcnt[0] % 2
        pcnt[0] += 1
        ps = MS[i]
        for c in range(nch):
            s = scratch()
            P.op("act", lambda e, s=s, c=c: e.activation(out=scb(s)[:, 0:TT], in_=src(c), func=AF.Square), rd=[sreg(c)], wr=[rS(s)])
            for ti, (c0, n) in enumerate(TILES):
                P.op("pe", lambda e, s=s, c=c, c0=c0, n=n: e.matmul(out=ps[:, c0:c0 + n], lhsT=ones_b, rhs=scb(s)[:, c0:c0 + n], start=(c == 0), stop=(c == nch - 1)),
                     rd=[rS(s), "ones"], wr=[bk(i, ti)])
        return rstd_from(ps, i, inv_n)

    def rstd_from(ps, i, inv_n):
        rs = scratch(pin=True)
        for ti, (c0, n) in enumerate(TILES):
            P.op("act", lambda e, c0=c0, n=n: e.activation(out=scf(rs)[:, c0:c0 + n], in_=ps[:, c0:c0 + n], func=AF.Sqrt, scale=inv_n, bias=epsc),
                 rd=["cons"], wr=[bk(i, ti), rS(rs)])
        P.op("dve", lambda e: e.reciprocal(out=scf(rs)[:, 0:TT], in_=scf(rs)[:, 0:TT]), wr=[rS(rs)])
        return rs

    nxt_rs = [None]

    def prenorm(l, gi):
        if nxt_rs[0] is not None:
            rs = nxt_rs[0]
            nxt_rs[0] = None
        else:
            rs = stats_rstd(lambda c: R[:, c, 0:TT], rA, 16, 1.0 / D)
        for c in range(16):
            P.op("dve", lambda e, c=c: e.scalar_tensor_tensor(out=hT[:, c, 0:TT], in0=R[:, c, 0:TT], scalar=pv(l * PPN + PP_G + gi * 16 + c), in1=scf(rs)[:, 0:TT], op0=ALU.mult, op1=ALU.mult),
                 rd=[rA(c), rS(rs), "pp"], wr=[rH(c)])
        pinned.discard(rs)
        P.op("sp", lambda e: e.dma_start(out=xpark[:, :, 0:TT], in_=R[:, :, 0:TT]), rd=[rA(c) for c in range(16)], wr=["xpark"], kind="d")

    def postnorm(l, gi):
        rs = stats_rstd(lambda c: R[:, c, 0:TT], rA, 16, 1.0 / D)
        fuse = not (l == NL - 1 and gi == 3)
        if fuse:
            i2 = pcnt[0] % 2
            pcnt[0] += 1
            ps2 = MS[i2]
        for c in range(16):
            xi = scratch()
            P.op("sp", lambda e, xi=xi, c=c: e.dma_start(out=scf(xi)[:, 0:TT], in_=xpark[:, c, 0:TT]), rd=["xpark"], wr=[rS(xi)], kind="d")
            P.op("dve", lambda e, c=c: e.scalar_tensor_tensor(out=R[:, c, 0:TT], in0=R[:, c, 0:TT], scalar=pv(l * PPN + PP_G + gi * 16 + c), in1=scf(rs)[:, 0:TT], op0=ALU.mult, op1=ALU.mult),
                 rd=[rS(rs), "pp"], wr=[rA(c)])
            P.op("dve", lambda e, xi=xi, c=c: e.tensor_tensor(out=R[:, c, 0:TT], in0=R[:, c, 0:TT], in1=scf(xi)[:, 0:TT], op=ALU.add), rd=[rS(xi)], wr=[rA(c)])
            if fuse:
                s2 = scratch()
                P.op("act", lambda e, s2=s2, c=c: e.activation(out=scb(s2)[:, 0:TT], in_=R[:, c, 0:TT], func=AF.Square), rd=[rA(c)], wr=[rS(s2)])
                for ti, (c0, n) in enumerate(TILES):
                    P.op("pe", lambda e, s2=s2, c=c, c0=c0, n=n: e.matmul(out=ps2[:, c0:c0 + n], lhsT=ones_b, rhs=scb(s2)[:, c0:c0 + n], start=(c == 0), stop=(c == 15)),
                         rd=[rS(s2), "ones"], wr=[bk(i2, ti)])
        pinned.discard(rs)
        if fuse:
            nxt_rs[0] = rstd_from(ps2, i2, 1.0 / D)

    def epi_to_R(mc):
        def epi(ps, i):
            evac2("act", lambda e, c0, n, p: e.activation(out=R[:, mc, c0:c0 + n], in_=p, func=AF.Copy), ps, i, [], [rA(mc)])
        return epi

    for t in range(9):
        bi = t % 2
        buf = ar[:, C0 + bi * 2048:C0 + (bi + 1) * 2048]
        rg = [("C", 2 * bi), ("C", 2 * bi + 1)]
        nrow = 128 if t < 8 else 1
        srcx = xp[t * 128:(t + 1) * 128, :] if t < 8 else xs[0:1, :]
        P.op("sp", lambda e, buf=buf, nrow=nrow, srcx=srcx: e.dma_start(out=buf[0:nrow, :], in_=srcx), wr=rg, kind="d")
        for c4 in range(4):
            px = PXY[c4 % 2]
            for cc in range(4):
                c = c4 * 4 + cc
                P.op("pe", lambda e, px=px, cc=cc, c=c, buf=buf, nrow=nrow: e.transpose(out=px[:, cc * 128:cc * 128 + nrow], in_=buf[0:nrow, c * 128:(c + 1) * 128], identity=ident_f[0:nrow, 0:nrow]),
                     rd=rg + ["cons"], wr=[RPX[c4 % 2]])
            P.op("act", lambda e, px=px, c4=c4, t=t, nrow=nrow: e.activation(out=R[:, c4 * 4:(c4 + 1) * 4, t * 128:t * 128 + nrow], in_=px[:, :].rearrange("p (c n) -> p c n", n=128)[:, :, 0:nrow], func=AF.Copy),
                 wr=[RPX[c4 % 2]] + [rA(c) for c in range(c4 * 4, c4 * 4 + 4)])

    isq = 1.0 / (128.0 ** 0.5)

    def layer(l):
        pl = l * PPN
        Wl = w_in[l]
        P.op("sp", lambda e, l=l: e.dma_start(out=bsb[:, :], in_=bsr_d[:, l * 512:(l + 1) * 512]), wr=["bsb"], kind="d")
        wn = scratch()
        P.op("sp", lambda e, l=l, wn=wn: e.dma_start(out=scf(wn)[:, 0:512].rearrange("p (g j) -> p g j", j=128), in_=w_s[l].rearrange("g i j -> i g j")), wr=[rS(wn)], kind="d")
        for g in range(4):
            P.op("pe", lambda e, g=g, wn=wn: e.transpose(out=PX[:, g * 128:(g + 1) * 128], in_=scf(wn)[:, g * 128:(g + 1) * 128], identity=ident_f), rd=[rS(wn), "cons"], wr=[RPX[0]])
        for g in range(4):
            P.op("dve", lambda e, g=g: e.tensor_tensor(out=wsT[:, g * 128:(g + 1) * 128], in0=PX[:, g * 128:(g + 1) * 128], in1=triu, op=ALU.mult), rd=["cons"], wr=[RPX[0], "wsT"])
        edma(stT, 0, [stc[l, 0, :], stc[l, 1, :]], True, aw=["stT"])

        prenorm(l, 0)
        if stage < 1:
            return

        for c in range(8):
            def epi_bv(ps, i, c=c):
                evac2("act", lambda e, c0, n, p: e.activation(out=Cf[:, c, c0:c0 + n], in_=p, func=AF.Copy), ps, i, [], [rCf(c)])
            mm_chunk(hpart(Wl, O_BV + c * 128), epi_bv)
        for c in range(8):
            s1 = scratch()
            P.op("act", lambda e, s1=s1, c=c: e.activation(out=scb(s1)[:, 0:TT], in_=Cf[:, c, 0:TT], func=AF.Copy), rd=[rCf(c)], wr=[rS(s1)])
            s2 = scratch()
            P.op("act", lambda e, s2=s2, c=c: e.activation(out=scb(s2)[:, 0:TT], in_=Cf[:, c, 0:TT], func=AF.Square), rd=[rCf(c)], wr=[rS(s2)])
            for ti, (c0, n) in enumerate(TILES):
                P.op("pe", lambda e, s1=s1, c=c, c0=c0, n=n: e.matmul(out=MS[0][:, c0:c0 + n], lhsT=ones_b, rhs=scb(s1)[:, c0:c0 + n], start=(c == 0), stop=(c == 7)), rd=[rS(s1), "ones"], wr=[bk(0, ti)])
                P.op("pe", lambda e, s2=s2, c=c, c0=c0, n=n: e.matmul(out=MS[1][:, c0:c0 + n], lhsT=ones_b, rhs=scb(s2)[:, c0:c0 + n], start=(c == 0), stop=(c == 7)), rd=[rS(s2), "ones"], wr=[bk(1, ti)])
        mu = scratch(pin=True)
        va = scratch(pin=True)
        for ti, (c0, n) in enumerate(TILES):
            P.op("act", lambda e, c0=c0, n=n: e.activation(out=scf(mu)[:, c0:c0 + n], in_=MS[0][:, c0:c0 + n], func=AF.Copy, scale=1.0 / 1024), wr=[bk(0, ti), rS(mu)])
            P.op("act", lambda e, c0=c0, n=n: e.activation(out=scf(va)[:, c0:c0 + n], in_=MS[1][:, c0:c0 + n], func=AF.Copy, scale=1.0 / 1024), wr=[bk(1, ti), rS(va)])
        tq = scratch()
        P.op("dve", lambda e: e.tensor_tensor(out=scf(tq)[:, 0:TT], in0=scf(mu)[:, 0:TT], in1=scf(mu)[:, 0:TT], op=ALU.mult), rd=[rS(mu)], wr=[rS(tq)])
        P.op("dve", lambda e: e.tensor_tensor(out=scf(va)[:, 0:TT], in0=scf(va)[:, 0:TT], in1=scf(tq)[:, 0:TT], op=ALU.subtract), rd=[rS(tq)], wr=[rS(va)])
        P.op("act", lambda e: e.activation(out=scf(va)[:, 0:TT], in_=scf(va)[:, 0:TT], func=AF.Sqrt, bias=epsc), rd=["cons"], wr=[rS(va)])
        P.op("dve", lambda e: e.reciprocal(out=scf(va)[:, 0:TT], in_=scf(va)[:, 0:TT]), wr=[rS(va)])
        for c in range(8):
            vf = scratch()
            P.op("dve", lambda e, vf=vf, c=c: e.tensor_tensor(out=scf(vf)[:, 0:TT], in0=Cf[:, c, 0:TT], in1=scf(mu)[:, 0:TT], op=ALU.subtract), rd=[rCf(c), rS(mu)], wr=[rS(vf)])
            P.op("dve", lambda e, vf=vf: e.tensor_tensor(out=scf(vf)[:, 0:TT], in0=scf(vf)[:, 0:TT], in1=scf(va)[:, 0:TT], op=ALU.mult), rd=[rS(va)], wr=[rS(vf)])
            P.op("dve", lambda e, vf=vf, c=c: e.tensor_scalar(out=scf(vf)[:, 0:TT], in0=scf(vf)[:, 0:TT], scalar1=pv(pl + PP_LNG + c), scalar2=pv(pl + PP_LNB + c), op0=ALU.mult, op1=ALU.add), rd=["pp"], wr=[rS(vf)])
            P.op("dve", lambda e, vf=vf, c=c: e.tensor_copy(out=vcs3[:, c, 0:1], in_=scf(vf)[:, 1024:1025]), rd=[rS(vf)], wr=["vcs"])
            P.op("dve", lambda e, vf=vf, c=c: e.tensor_scalar(out=Cf[:, c, 1024:1025], in0=scf(vf)[:, 1024:1025], scalar1=pv(pl + PP_WS0 + c), scalar2=pv(pl + PP_BS0 + c), op0=ALU.mult, op1=ALU.add), rd=[rS(vf), "pp"], wr=[rCf(c)])
            vb = scratch()
            P.op("act", lambda e, vf=vf, vb=vb: e.activation(out=scb(vb)[:, 0:1024], in_=scf(vf)[:, 0:1024], func=AF.Copy), rd=[rS(vf)], wr=[rS(vb)])
            pxb = bf(PX[:, :])
            for t in range(8):
                P.op("pe", lambda e, vb=vb, t=t, pxb=pxb: e.transpose(out=pxb[:, t * 128:(t + 1) * 128], in_=scb(vb)[:, t * 128:(t + 1) * 128], identity=ident_b), rd=[rS(vb), "identb"], wr=[RPX[0]])
            P.op("act", lambda e, c=c, pxb=pxb: e.activation(out=vn_tm[:, 0:8, c * 128:(c + 1) * 128], in_=pxb.rearrange("p (t f) -> p t f", f=128), func=AF.Copy), wr=[RPX[0]] + RVN)
        pinned.discard(mu)
        pinned.discard(va)
        edma(vcs3, 0, [vch_o[l, 0, :]], False, rd=["vcs"], aw=[("vch", l)])
        for fc in range(8):
            g = fc // 2
            for hf in range(2):
                px = PXY[hf]
                for tt in range(4):
                    t = hf * 4 + tt
                    P.op("pe", lambda e, px=px, tt=tt, t=t, fc=fc, g=g: e.matmul(out=px[:, tt * 128:(tt + 1) * 128], lhsT=vn_tm[:, t, fc * 128:(fc + 1) * 128], rhs=wsT[:, g * 128:(g + 1) * 128], start=True, stop=True),
                         rd=RVN + ["wsT"], wr=[RPX[hf]])
                for tt in range(4):
                    P.op("dve", lambda e, px=px, tt=tt, hf=hf, fc=fc, g=g: e.tensor_tensor(out=Cf[:, fc, hf * 512 + tt * 128:hf * 512 + (tt + 1) * 128], in0=px[:, tt * 128:(tt + 1) * 128], in1=bsb[:, g * 128:(g + 1) * 128], op=ALU.add),
                         rd=["bsb"], wr=[RPX[hf], rCf(fc)])
        for c in range(8):
            def epi_bu(ps, i, c=c):
                evac2("dve", lambda e, c0, n, p: e.tensor_tensor(out=yb[:, c, c0:c0 + n], in0=p, in1=Cf[:, c, c0:c0 + n], op=ALU.mult), ps, i, [rCf(c)], [rYB(c)])
            mm_chunk(hpart(Wl, O_BU + c * 128), epi_bu)

        if stage < 2:
            return
        for mc in range(36):
            typ, hh = mc // 12, mc % 12
            g, h = hh // 4, hh % 4

            def epi_qkv(ps, i, typ=typ, hh=hh, g=g, h=h, l=l):
                fm = scratch(pin=True)
                evac2("act", lambda e, c0, n, p: e.activation(out=scf(fm)[:, c0:c0 + n], in_=p, func=AF.Copy), ps, i, [], [rS(fm)])
                return lambda: post_qkv(ps, i, fm, typ, hh, g, h, l)

            def post_qkv(ps, i, fm, typ, hh, g, h, l):
                for t in range(8):
                    px = PXY[t // 4]
                    P.op("pe", lambda e, px=px, t=t: e.transpose(out=px[:, (t % 4) * 128:(t % 4 + 1) * 128], in_=scf(fm)[:, t * 128:(t + 1) * 128], identity=ident_f), rd=[rS(fm), "cons"], wr=[RPX[t // 4]])
                P.op("pe", lambda e: e.transpose(out=ps[0:1, 1152:1280], in_=scf(fm)[:, 1024:1025], identity=ident_f), rd=[rS(fm), "cons"], wr=[bk(i, 2)])
                tm = scratch()
                tm3 = scf(tm).rearrange("p (t d) -> p t d", d=128)
                for hf in range(2):
                    P.op("act", lambda e, hf=hf: e.activation(out=tm3[:, hf * 4:(hf + 1) * 4, :], in_=PXY[hf][:, :].rearrange("p (t d) -> p t d", d=128), func=AF.Copy), wr=[RPX[hf], rS(tm)])
                P.op("act", lambda e: e.activation(out=tm3[0:1, 8, :], in_=ps[0:1, 1152:1280], func=AF.Copy), wr=[bk(i, 2), rS(tm)])
                if typ < 2:
                    tp = scratch()
                    t4 = scf(tp)[:, 0:576].rearrange("p (a t j) -> p a t j", a=4, j=16)
                    x1, x2 = tm3[:, :, 0:16], tm3[:, :, 16:32]
                    for a, (xa, tb) in enumerate([(x1, cos_t), (x2, sin_t), (x2, cos_t), (x1, sin_t)]):
                        P.op("dve", lambda e, a=a, xa=xa, tb=tb: e.tensor_tensor(out=t4[:, a, :, :], in0=xa, in1=tb, op=ALU.mult), rd=[rS(tm), "cons"], wr=[rS(tp)])
                    P.op("dve", lambda e: e.tensor_tensor(out=x1, in0=t4[:, 0, :, :], in1=t4[:, 1, :, :], op=ALU.subtract), rd=[rS(tp)], wr=[rS(tm)])
                    P.op("dve", lambda e: e.tensor_tensor(out=x2, in0=t4[:, 2, :, :], in1=t4[:, 3, :, :], op=ALU.add), rd=[rS(tp)], wr=[rS(tm)])
                if typ >= 1:
                    off = (typ - 1) * 512 + h * 128
                    P.op("sp", lambda e: e.dma_start(out=kvout[l, 0:1024, g, off:off + 128].rearrange("(t p) d -> p t d", p=128), in_=tm3[:, 0:8, :]), rd=[rS(tm)], aw=[("kvout", l)], kind="d")
                    P.op("sp", lambda e: e.dma_start(out=kvout[l, 1024:1025, g, off:off + 128], in_=tm3[0:1, 8, :]), rd=[rS(tm)], aw=[("kvout", l)], kind="d")
                    for (kch, ta, tb2) in [[(3, 7, 8)], [(2, 4, 8)], [(0, 0, 4), (1, 4, 8)]][g]:
                        P.op("sp", lambda e, kch=kch, ta=ta, tb2=tb2: e.dma_start(out=snd[l][kch][0:(tb2 - ta) * 128, off:off + 128].rearrange("(t p) d -> p t d", p=128), in_=tm3[:, ta:tb2, :]), rd=[rS(tm)], aw=[("snd", l, kch)], kind="d")
                else:
                    tb_ = scratch()
                    P.op("dve", lambda e: e.tensor_copy(out=scb(tb_)[:, 0:1152], in_=scf(tm)[:, 0:1152]), rd=[rS(tm)], wr=[rS(tb_)])
                    pxb = bf(PX[:, :])
                    for t in range(8):
                        P.op("pe", lambda e, t=t: e.transpose(out=pxb[:, t * 128:(t + 1) * 128], in_=scb(tb_)[:, t * 128:(t + 1) * 128], identity=ident_b), rd=[rS(tb_), "identb"], wr=[RPX[0]])
                    P.op("act", lambda e: e.activation(out=qT[:, hh, 0:1024], in_=pxb, func=AF.Copy), wr=[RPX[0], rQ(hh)])
                    P.op("pe", lambda e: e.transpose(out=ps[:, 1040:1041], in_=tm3[0:1, 8, :], identity=ident_f[0:1, 0:1]), rd=[rS(tm), "cons"], wr=[bk(i, 2)])
                    P.op("act", lambda e: e.activation(out=qT[:, hh, 1024:1025], in_=ps[:, 1040:1041], func=AF.Copy), wr=[bk(i, 2), rQ(hh)])
                pinned.discard(fm)
            mm_chunk(hpart(Wl, O_Q + mc * 128), epi_qkv)
        flush_pend()

        for kch in range(4):
            P.op("pool", lambda e, l=l, kch=kch: e.collective_compute("AllGather", ALU.bypass, replica_groups=[[0, 1], [2, 3], [4, 5], [6, 7]], ins=[snd[l][kch].opt()], outs=[rcv[l][kch].opt()]),
                 rd=[("snd", l, kch)], wr=[("rcv", l, kch)], kind="cc", dsem=kch)
        if stage < 3:
            return
        w0c, w1c, w2c = pl + PP_CW, pl + PP_CW + 8, pl + PP_CW + 16
        for c in range(8):
            act_s = scratch(pin=True)
            zc = scratch(pin=True)
            cv = scratch(pin=True)

            def epi_ac(ps, i, act_s=act_s):
                evac2("act", lambda e, c0, n, p: e.activation(out=scf(act_s)[:, c0:c0 + n], in_=p, func=AF.Copy), ps, i, [], [rS(act_s)])

            def epi_ax(ps, i, act_s=act_s, zc=zc, c=c, cv=cv):
                P.op("dve", lambda e: e.memset(scf(zc)[:, 0:2], 0.0), wr=[rS(zc)])
                P.op("dve", lambda e: e.tensor_copy(out=scf(zc)[:, 1026:1028], in_=stT[:, c, 0:2]), rd=["stT"], wr=[rS(zc)])
                for ti, (c0, n) in enumerate(TILES):
                    o0 = c0 + 2 if ti < 2 else 1028
                    P.op("dve", lambda e, c0=c0, n=n, o0=o0: e.tensor_tensor(out=scf(zc)[:, o0:o0 + n], in0=ps[:, c0:c0 + n], in1=scf(act_s)[:, c0:c0 + n], op=ALU.mult), rd=[rS(act_s)], wr=[bk(i, ti), rS(zc)])
                P.op("dve", lambda e: e.tensor_copy(out=zt[:, c, 0:2], in_=scf(zc)[:, 1024:1026]), rd=[rS(zc)], wr=["zt"])
                P.op("dve", lambda e: e.tensor_copy(out=zt[:, c, 2:4], in_=scf(zc)[:, 1027:1029]), rd=[rS(zc)], wr=["zt"])
                P.op("dve", lambda e: e.tensor_scalar(out=scf(cv)[:, 0:1027], in0=scf(zc)[:, 0:1027], scalar1=pv(w0c + c), scalar2=None, op0=ALU.mult), rd=[rS(zc), "pp"], wr=[rS(cv)])
                P.op("dve", lambda e: e.scalar_tensor_tensor(out=scf(cv)[:, 0:1027], in0=scf(zc)[:, 1:1028], scalar=pv(w1c + c), in1=scf(cv)[:, 0:1027], op0=ALU.mult, op1=ALU.add), rd=[rS(zc)], wr=[rS(cv)])
                P.op("dve", lambda e: e.scalar_tensor_tensor(out=scf(cv)[:, 0:1027], in0=scf(zc)[:, 2:1029], scalar=pv(w2c + c), in1=scf(cv)[:, 0:1027], op0=ALU.mult, op1=ALU.add), rd=[rS(zc)], wr=[rS(cv)])

            def epi_ab(ps, i, zc=zc, c=c, cv=cv):
                for ti, (c0, n) in enumerate(TILES):
                    o0 = c0 if ti < 2 else 1026
                    P.op("dve", lambda e, c0=c0, n=n, o0=o0: e.tensor_tensor(out=ya[:, c, c0:c0 + n], in0=ps[:, c0:c0 + n], in1=scf(cv)[:, o0:o0 + n], op=ALU.mult), rd=[rS(cv)], wr=[bk(i, ti), rYA(c)])
                P.op("dve", lambda e: e.tensor_copy(out=abh[:, c, 0:2], in_=ps[:, 0:2]), wr=[bk(i, 0), "abh"])
                P.op("dve", lambda e: e.tensor_tensor(out=yah[:, c, 0:2], in0=ps[:, 0:2], in1=scf(cv)[:, 0:2], op=ALU.mult), rd=[rS(cv)], wr=[bk(i, 0), "yah"])
            mm_chunk(hpart(Wl, O_AC + c * 128), epi_ac)
            mm_chunk(hpart(Wl, O_AX + c * 128), epi_ax)
            mm_chunk(hpart(Wl, O_AB + c * 128), epi_ab)
            pinned.discard(act_s)
            pinned.discard(zc)
            pinned.discard(cv)
        edma(zt, 0, [snd[l][4][0, :], snd[l][4][1, :]], False, rd=["zt"], aw=[("snd", l, 4)])
        edma(zt, 0, [cst_o[l, r_, :] for r_ in range(4)], False, rd=["zt"], aw=[("cst", l)])

        if stage < 4:
            return
        P.op("pool", lambda e, l=l: e.collective_compute("AllGather", ALU.bypass, replica_groups=[[0, 1], [2, 3], [4, 5], [6, 7]], ins=[snd[l][4].opt()], outs=[rcv[l][4].opt()]),
             rd=[("snd", l, 4)], wr=[("rcv", l, 4)], kind="cc", dsem=4)
        if stage < 5:
            return
        def own(g_, r0, step, n=128):
            return kvout[l, r0:r0 + (n - 1) * step + 1:step, g_, :], [("kvout", l)]

        def prv(kch, r0, step, n=128):
            return rcv[l][kch][r0:r0 + (n - 1) * step + 1:step, :], [("rcv", l, kch)]

        specs = []
        specs.append((0, [prv(3, 0, 1) + (0, 128)], (256, 128), (0, 1, 128)))
        for kb in range(8):
            if kb < 7:
                specs.append((0, [own(0, kb * 128, 1) + (0, 128)], (0, 256), (kb * 128, 1, 256)))
            else:
                specs.append((0, [own(0, kb * 128, 1) + (0, 128)], (0, 128), (kb * 128, 1, 128)))
        for r in range(4):
            specs.append((1, [prv(2, r, 4) + (0, 128)], (256, 128), (r, 4, 128)))
            specs.append((1, [own(1, r, 4) + (0, 128)], (0, 256), (r, 4, 256)))
            specs.append((1, [own(1, 512 + r, 4) + (0, 128)], (0, 128), (512 + r, 4, 128)))
        for r in range(16):
            specs.append((2, [prv(0, r, 16, 32) + (0, 32), prv(1, r, 16, 32) + (32, 32), own(2, r, 16, 64) + (64, 64)], (384, 64), (r, 16, 64)))
        for g_ in range(3):
            dil = [1, 4, 16][g_]
            specs.append((g_, [(ck[g_][l, 0:127 * dil + 1:dil, :], [], 0, 128)], None, (1024, 1, 1)))
            specs.append((g_, [own(g_, 1024, 1, 1) + (0, 1)], (448, 1), (1024, 1, 1)))

        steps = []
        for (g_, srcs, mk, (q0, qs, N)) in specs:
            if N == 256:
                steps.append((g_, srcs, (mk[0], 128), (q0, qs, 128)))
                steps.append((g_, srcs, (mk[0] + 128, 128), (q0 + 128 * qs, qs, 128)))
            else:
                steps.append((g_, srcs, mk, (q0, qs, N)))
        UZ4 = Cf[:, 0:8, :]
        RUZ4 = [rCf(c_) for c_ in range(8)]
        P.op("pool", lambda e: e.memset(UZ4[:, :, 0:TT], 0.0), wr=RUZ4)
        ctx = {}

        def stA(si):
            (g_, srcs, mk, (q0, qs, N)) = steps[si]
            par = si % 2
            kb_ = scratch()
            kb2 = scb(kb_)[:, 0:1024]
            kts = scb(kb_)[:, 1024:1536]
            pt = scb(kb_)[:, 1536:2048]
            for (sap, sreg, p0, pn) in srcs:
                P.op("pool", lambda e, sap=sap, p0=p0, pn=pn: e.dma_start(out=kb2[p0:p0 + pn, :], in_=sap), rd=sreg, aw=[rS(kb_)], kind="d")
            ms = MS[par]
            ktp = bf(ms[:, 1024:1536])
            for h in range(4):
                P.op("pe", lambda e, h=h: e.transpose(out=ktp[:, h * 128:(h + 1) * 128], in_=kb2[:, h * 128:(h + 1) * 128], identity=ident_b), rd=[rS(kb_), "identb"], wr=[bk(par, 2)])
            P.op("act", lambda e: e.activation(out=kts, in_=ktp[:, 0:512], func=AF.Copy), wr=[bk(par, 2), rS(kb_)])
            ctx[si] = (par, kb_, kb2, kts, pt, ms)

        def stB(si):
            (g_, srcs, mk, (q0, qs, N)) = steps[si]
            (par, kb_, kb2, kts, pt, ms) = ctx[si]
            px = PXY[par]
            for h in range(4):
                qap = qT[:, g_ * 4 + h, q0:q0 + (N - 1) * qs + 1:qs]
                P.op("pe", lambda e, qap=qap, h=h: e.matmul(out=px[:, h * N:(h + 1) * N], lhsT=kts[:, h * 128:(h + 1) * 128], rhs=qap, start=True, stop=(mk is None)), rd=[rS(kb_), rQ(g_ * 4 + h)], wr=[RPX[par]])
                if mk is not None:
                    P.op("pe", lambda e, h=h: e.matmul(out=px[:, h * N:(h + 1) * N], lhsT=ident_b, rhs=maskb[:, mk[0]:mk[0] + N], start=False, stop=True), rd=["identb", "maskb"], wr=[RPX[par]])
            P.op("act", lambda e: e.activation(out=pt[:, 0:4 * N], in_=px[:, 0:4 * N], func=AF.Exp, scale=isq), wr=[RPX[par], rS(kb_)])

        def stC(si):
            (g_, srcs, mk, (q0, qs, N)) = steps[si]
            (par, kb_, kb2, kts, pt, ms) = ctx.pop(si)
            for h in range(4):
                bkh = bk(par, h // 2)
                P.op("pe", lambda e, h=h: e.matmul(out=ms[:, (2 * h) * 128:(2 * h) * 128 + N], lhsT=kb2[:, 512 + h * 128:512 + (h + 1) * 128], rhs=pt[:, h * N:(h + 1) * N], start=True, stop=True), rd=[rS(kb_)], wr=[bkh])
                P.op("pe", lambda e, h=h: e.matmul(out=ms[:, (2 * h + 1) * 128:(2 * h + 1) * 128 + N], lhsT=ones_b, rhs=pt[:, h * N:(h + 1) * N], start=True, stop=True), rd=[rS(kb_), "ones"], wr=[bkh])
            uz = UZ4[:, :, q0:q0 + (N - 1) * qs + 1:qs]
            P.op("dve", lambda e: e.tensor_tensor(out=uz, in0=uz, in1=ms[:, 0:1024].rearrange("p (c n) -> p c n", n=128)[:, :, 0:N], op=ALU.add), wr=[bk(par, 0), bk(par, 1)] + RUZ4)

        nst = len(steps)
        for it in range(nst + 2):
            if it < nst:
                stA(it)
            if 0 <= it - 1 < nst:
                stB(it - 1)
            if 0 <= it - 2 < nst:
                stC(it - 2)
        for h in range(4):
            rc = scratch()
            P.op("dve", lambda e, rc=rc, h=h: e.reciprocal(out=scf(rc)[:, 0:TT], in_=UZ4[:, 2 * h + 1, 0:TT]), rd=RUZ4, wr=[rS(rc)])
            P.op("dve", lambda e, rc=rc, h=h: e.tensor_tensor(out=yc[:, h, 0:TT], in0=UZ4[:, 2 * h, 0:TT], in1=scf(rc)[:, 0:TT], op=ALU.mult), rd=RUZ4 + [rS(rc)], wr=[rYC(h)])

        edma(hist, 0, [rcv[l][4][0, :], rcv[l][4][1, :]], True, rd=[("rcv", l, 4)], aw=["hist"])
        P.op("dve", lambda e: e.tensor_scalar(out=hist[:, :, 0:2], in0=hist[:, :, 0:2], scalar1=flag, scalar2=None, op0=ALU.mult), rd=["cons"], wr=["hist"])
        W0, W1 = pp[:, w0c:w0c + 8], pp[:, w1c:w1c + 8]
        h0, h1 = hist[:, :, 0], hist[:, :, 1]
        t1, t2 = hist[:, :, 2], hist[:, :, 3]
        ops = [
            (t1, W0, h0, ALU.mult), (t2, W1, h1, ALU.mult), (t1, t1, t2, ALU.add), (t1, t1, abh[:, :, 0], ALU.mult), (t1, t1, yah[:, :, 0], ALU.add),
            (t2, W0, h1, ALU.mult), (t2, t2, abh[:, :, 1], ALU.mult), (t2, t2, yah[:, :, 1], ALU.add),
        ]
        for (o_, a_, b_, op_) in ops:
            P.op("dve", lambda e, o_=o_, a_=a_, b_=b_, op_=op_: e.tensor_tensor(out=o_, in0=a_, in1=b_, op=op_), rd=["pp", "abh", "yah"], wr=["hist"])
        P.op("dve", lambda e: e.tensor_copy(out=ya[:, :, 0], in_=t1), rd=["hist"], wr=[rYA(c) for c in range(8)])
        P.op("dve", lambda e: e.tensor_copy(out=ya[:, :, 1], in_=t2), rd=["hist"], wr=[rYA(c) for c in range(8)])

        if stage < 6:
            return
        for mc in range(16):
            macc = scratch(pin=True)
            for bi, (gcol, wout, kc, actT, areg) in enumerate([(O_GA, w_a_out, 8, ya, rYA), (O_GB, w_b_out, 8, yb, rYB), (O_GC, w_c_out, 4, yc, rYC)]):
                sg = scratch(pin=True)

                def epi_g(ps, i, sg=sg):
                    evac2("act", lambda e, c0, n, p: e.activation(out=scf(sg)[:, c0:c0 + n], in_=p, func=AF.Sigmoid), ps, i, [], [rS(sg)])

                def epi_o(ps, i, sg=sg, bi=bi, macc=macc, mc=mc):
                    if bi == 0:
                        evac2("dve", lambda e, c0, n, p: e.tensor_tensor(out=scf(macc)[:, c0:c0 + n], in0=p, in1=scf(sg)[:, c0:c0 + n], op=ALU.mult), ps, i, [rS(sg)], [rS(macc)])
                    else:
                        evac2("dve", lambda e, c0, n, p: e.tensor_tensor(out=scf(sg)[:, c0:c0 + n], in0=p, in1=scf(sg)[:, c0:c0 + n], op=ALU.mult), ps, i, [], [rS(sg)])
                        if bi == 1:
                            P.op("dve", lambda e: e.tensor_tensor(out=scf(macc)[:, 0:TT], in0=scf(macc)[:, 0:TT], in1=scf(sg)[:, 0:TT], op=ALU.add), rd=[rS(sg)], wr=[rS(macc)])
                        else:
                            P.op("dve", lambda e: e.tensor_tensor(out=mT[:, mc, 0:TT], in0=scf(macc)[:, 0:TT], in1=scf(sg)[:, 0:TT], op=ALU.add), rd=[rS(sg), rS(macc)], wr=[rM(mc)])
                mm_chunk(hpart(Wl, gcol + mc * 128), epi_g)
                mm_chunk([(wout[l], 0, kc, mc * 128, actT, areg, 0)], epi_o)
                pinned.discard(sg)
            pinned.discard(macc)

        if stage < 7:
            return
        for mc in range(16):
            mm_chunk([(w_o[l], 0, 16, mc * 128, mT, rM, 0)], epi_to_R(mc))
        postnorm(l, 1)

        if stage < 8:
            return
        prenorm(l, 2)
        for hf in range(2):
            for j in range(22):
                jj = hf * 22 + j
                sg = scratch(pin=True)

                def epi_gate(ps, i, sg=sg):
                    evac2("act", lambda e, c0, n, p: e.activation(out=scf(sg)[:, c0:c0 + n], in_=p, func=AF.Silu), ps, i, [], [rS(sg)])

                def epi_up(ps, i, sg=sg, j=j):
                    evac2("dve", lambda e, c0, n, p: e.tensor_tensor(out=hid[:, j, c0:c0 + n], in0=p, in1=scf(sg)[:, c0:c0 + n], op=ALU.mult), ps, i, [rS(sg)], [rHid(j)])
                mm_chunk(hpart(w_f1[l], jj * 128), epi_gate)
                mm_chunk(hpart(w_f1[l], DFF + jj * 128), epi_up)
                pinned.discard(sg)
            for mc in range(16):
                parts = [(w_f2[l], hf * 22, 11, mc * 128, hid, rHid, -hf * 22), (w_f2[l], hf * 22 + 11, 11, mc * 128, hid, rHid, -hf * 22)]
                if hf == 0:
                    mm_chunk(parts, epi_to_R(mc))
                else:
                    def epi_acc(ps, i, mc=mc):
                        evac2("dve", lambda e, c0, n, p: e.tensor_tensor(out=R[:, mc, c0:c0 + n], in0=p, in1=R[:, mc, c0:c0 + n], op=ALU.add), ps, i, [], [rA(mc)])
                    mm_chunk(parts, epi_acc)
        postnorm(l, 3)

    for l_ in range(NL):
        layer(l_)

    outs = []
    for t in range(9):
        bi = t % 2
        buf = ar[:, C0 + bi * 2048:C0 + (bi + 1) * 2048]
        rg = [("C", 2 * bi), ("C", 2 * bi + 1)]
        nrow = 128 if t < 8 else 1
        for c4 in range(4):
            px = PXY[c4 % 2]
            for cc in range(4):
                c = c4 * 4 + cc
                P.op("pe", lambda e, px=px, cc=cc, c=c, t=t, nrow=nrow: e.transpose(out=px[0:nrow, cc * 128:(cc + 1) * 128], in_=R[:, c, t * 128:t * 128 + nrow], identity=ident_f), rd=[rA(c), "cons"], wr=[RPX[c4 % 2]])
            P.op("act", lambda e, px=px, c4=c4, buf=buf, nrow=nrow: e.activation(out=buf[0:nrow, c4 * 512:(c4 + 1) * 512], in_=px[0:nrow, :], func=AF.Copy), wr=[RPX[c4 % 2]] + rg)
        outs.append(P.op("sp", lambda e, buf=buf, nrow=nrow, t=t: e.dma_start(out=y_o[t * 128:t * 128 + nrow, :], in_=buf[0:nrow, :]), rd=rg, aw=["yout"], kind="d"))
    fin = list(outs)
    for key in list(P.reg.keys()):
        if isinstance(key, tuple) and key[0] in ("kvout", "cst", "vch"):
            fin += P.reg[key][0]
    P.op("sp", None, extra=fin)

    nsem = {}
    eng_sems = {e: nc.alloc_semaphore("sem_" + e) for e in Prog.ENGS}
    dma_sems = [nc.alloc_semaphore("sem_dma%d" % i) for i in range(NDMA + NWB)]
    cc_sem = [nc.alloc_semaphore("sem_cc%d" % i) for i in range(5)]
    P.emit(nc, eng_sems, dma_sems, cc_sem)
    return nc


_NC = None


def kernel(**inputs):
    return _run(inputs, L, 99)


def _run(inputs, NL, stage):
    global _NC
    f = lambda k: np.ascontiguousarray(np.asarray(inputs[k], dtype=np.float32))
    x_prompt, x_sample = f("x_prompt"), f("x_sample")
    state_conv = f("state_conv")
    c128, c512, c2048 = f("cache_kv_w128"), f("cache_kv_w512"), f("cache_kv_w2048")
    W = {k: np.ascontiguousarray(f(k)[:NL]) for k in ["w_in", "w_a_out", "w_b_out", "w_c_out", "w_o", "w_ffn_in", "w_ffn_out", "w_s"]}
    pp = np.zeros((128, L, PPN), np.float32)
    for gi, k in enumerate(["g_pre_mix", "g_post_mix", "g_pre_ffn", "g_post_ffn"]):
        pp[:, :, PP_G + gi * 16:PP_G + (gi + 1) * 16] = f(k).reshape(L, 16, 128).transpose(2, 0, 1)
    cw = f("conv_w").reshape(L, 3, 8, 128)
    pp[:, :, PP_CW:PP_CW + 24] = cw.transpose(3, 0, 1, 2).reshape(128, L, 24)
    pp[:, :, PP_LNG:PP_LNG + 8] = f("ln_g").reshape(L, 8, 128).transpose(2, 0, 1)
    pp[:, :, PP_LNB:PP_LNB + 8] = f("ln_b").reshape(L, 8, 128).transpose(2, 0, 1)
    ws = f("w_s")
    bs = f("b_s")
    for c in range(8):
        pp[:, :, PP_WS0 + c] = ws[:, c // 2, 0, 0][None, :]
        pp[:, :, PP_BS0 + c] = bs[:, c // 2, 0][None, :]
    pp = np.ascontiguousarray(pp.reshape(128, L * PPN))
    bsr = np.ascontiguousarray(np.broadcast_to(bs.reshape(1, L * 512), (128, L * 512)))
    NEG = -30000.0
    jj = np.arange(128)[:, None]
    ii = np.arange(128)[None, :]
    diag = np.where(jj <= ii, 0.0, NEG).astype(np.float32)
    prevb = np.where(jj >= ii, 0.0, NEG).astype(np.float32)
    inv_freq = (np.float32(500000.0) ** (-np.arange(0, 32, 2, dtype=np.float32) / np.float32(32))).astype(np.float32)
    in_maps = []
    for c in range(8):
        b, half = c // 2, c % 2
        cp = np.zeros((128, CPN), np.float32)
        cp[:, 0:128] = np.eye(128, dtype=np.float32)
        cp[:, 128:256] = (jj <= ii).astype(np.float32)
        cp[:, 256:384] = diag
        cp[:, 384:512] = prevb
        cp[:, 512:640] = prevb if half == 1 else NEG
        g2 = np.where(np.arange(128)[:, None] <= 64 + np.arange(64)[None, :], 0.0, NEG).astype(np.float32)
        if half == 0:
            g2[0:64, :] = NEG
        cp[:, 640:704] = g2
        cp[:, 704] = NEG
        cp[0, 704] = 0.0
        pos = np.zeros((128, 9), np.float32)
        for t in range(8):
            pos[:, t] = half * 1024 + t * 128 + np.arange(128)
        pos[:, 8] = 16384.0
        ang = pos[:, :, None].astype(np.float32) * inv_freq[None, None, :]
        cp[:, 768:912] = np.cos(ang).astype(np.float32).reshape(128, 144)
        cp[:, 912:1056] = np.sin(ang).astype(np.float32).reshape(128, 144)
        cp[:, 1056] = float(half)
        cp[:, 1057] = EPS
        m = {
            "xp": np.ascontiguousarray(x_prompt[b, half * 1024:(half + 1) * 1024]),
            "xs": np.ascontiguousarray(x_sample[c]),
            "stc": np.ascontiguousarray(state_conv[:NL, c]),
            "ck128": np.ascontiguousarray(c128[:NL, c].reshape(NL, 128, 1024)),
            "ck512": np.ascontiguousarray(c512[:NL, c].reshape(NL, 512, 1024)),
            "ck2048": np.ascontiguousarray(c2048[:NL, c].reshape(NL, 2048, 1024)),
            "pp": pp, "bsr": bsr, "cpack": cp,
        }
        m.update(W)
        in_maps.append(m)
    if NL == L and stage == 99:
        if _NC is None:
            _NC = build_nc()
        ncx = _NC
    else:
        ncx = build_nc(NL, stage)
    res = run_bass_kernel_spmd(ncx, in_maps, core_ids=list(range(8)))
    R_ = res.results
    y_prompt = np.zeros((4, 2048, D), np.float32)
    y_sample = np.zeros((8, 1, D), np.float32)
    csp = np.zeros((L, 4, 2, 1024), np.float32)
    k128 = np.zeros((L, 4, 128, 2, 4, 128), np.float32)
    k512 = np.zeros((L, 4, 512, 2, 4, 128), np.float32)
    k2048 = np.zeros((L, 4, 2048, 2, 4, 128), np.float32)
    css = np.zeros((L, 8, 2, 1024), np.float32)
    s128 = np.zeros((L, 8, 1, 2, 4, 128), np.float32)
    s512 = np.zeros((L, 8, 1, 2, 4, 128), np.float32)
    s2048 = np.zeros((L, 8, 1, 2, 4, 128), np.float32)
    vcs = np.zeros((L, 8, 1, 1024), np.float32)
    for c in range(8):
        b, half = c // 2, c % 2
        r = R_[c]
        y_prompt[b, half * 1024:(half + 1) * 1024] = r["y"][0:1024]
        y_sample[c, 0] = r["y"][1024]
        kv = r["kvout"]
        k2048[:, b, half * 1024:(half + 1) * 1024] = kv[:, 0:1024, 2].reshape(L, 1024, 2, 4, 128)
        if half == 1:
            csp[:, b] = r["cst"][:, 0:2]
            k128[:, b] = kv[:, 896:1024, 0].reshape(L, 128, 2, 4, 128)
            k512[:, b] = kv[:, 512:1024, 1].reshape(L, 512, 2, 4, 128)
        css[:, c] = r["cst"][:, 2:4]
        s128[:, c, 0] = kv[:, 1024, 0].reshape(L, 2, 4, 128)
        s512[:, c, 0] = kv[:, 1024, 1].reshape(L, 2, 4, 128)
        s2048[:, c, 0] = kv[:, 1024, 2].reshape(L, 2, 4, 128)
        vcs[:, c, 0] = r["vch"][:, 0]
    return (y_prompt, y_sample, csp, k128, k512, k2048, css, s128, s512, s2048, vcs)
```
